# Optimizing a Trainium2 kernel written in Bass

```python
import jax, jax.numpy as jnp
from jax import lax
import numpy as np

D_MODEL = 1024
BATCH = 32
SEQ = 2048
DEPTH = 2

HG_HEADS = 4
HG_DIM = 128
HG_WIDTH = HG_HEADS * HG_DIM
HG_CHUNK = 64
NSA_HEADS = 8
NSA_KV_HEADS = 2
NSA_GROUP = NSA_HEADS // NSA_KV_HEADS
NSA_DIM = 64
NSA_WIDTH = NSA_HEADS * NSA_DIM
NSA_KV_WIDTH = NSA_KV_HEADS * NSA_DIM
CMP_LEN = 32
CMP_STRIDE = 16
CMP_HIDDEN = 256
SLC_LEN = 64
SLC_TOP = 16
SLC_QBLOCK = 32
WINDOW = 512
SWA_QBLOCK = 128
N_BRANCH = 3
MIX_WIDTH = HG_WIDTH + NSA_WIDTH
D_FF = ((8 * D_MODEL + 3 * 256 - 1) // (3 * 256)) * 256
ROPE_THETA = 10000.0
RMS_EPS = 1e-6
NEG = -1e30
FORCED_SCORE = 1e9
IN_SPLITS = (HG_WIDTH, HG_WIDTH, HG_WIDTH, HG_WIDTH, NSA_WIDTH,
             NSA_KV_WIDTH, NSA_KV_WIDTH, NSA_KV_WIDTH, NSA_KV_WIDTH, NSA_KV_WIDTH, NSA_KV_WIDTH,
             NSA_HEADS * N_BRANCH)
IN_WIDTH = sum(IN_SPLITS)
IN_SPLIT_POINTS = tuple(int(v) for v in np.cumsum(IN_SPLITS)[:-1])

kernel_name = 'hymba_hgrn2_nsa_swiglu_trunk'


def rms_norm(x, gain):
    xf = x.astype(jnp.float32)
    y = xf * lax.rsqrt(jnp.mean(xf * xf, axis=-1, keepdims=True) + RMS_EPS)
    return (y * gain.astype(jnp.float32)).astype(x.dtype)


def rope(x, pos):
    half = x.shape[-1] // 2
    freqs = ROPE_THETA ** (-jnp.arange(half, dtype=jnp.float32) / half)
    ang = pos.astype(jnp.float32)[:, None] * freqs[None, :]
    cos = jnp.cos(ang)[:, None, :]
    sin = jnp.sin(ang)[:, None, :]
    xf = x.astype(jnp.float32)
    x1, x2 = xf[..., :half], xf[..., half:]
    return jnp.concatenate([x1 * cos - x2 * sin, x1 * sin + x2 * cos], axis=-1).astype(x.dtype)


def hgrn2_mixer(q, f_logit, i, g, lower_bound, g_norm):
    B, T, _ = q.shape
    dt = q.dtype
    n_chunk = T // HG_CHUNK
    lb = lower_bound.astype(jnp.float32)
    f = lb + (1.0 - lb) * jax.nn.sigmoid(f_logit.astype(jnp.float32))
    log_f = jnp.log(f)
    k = 1.0 - f
    qf = jax.nn.silu(q.astype(jnp.float32))
    v = i.astype(jnp.float32)

    def to_chunks(a):
        return a.reshape(B, n_chunk, HG_CHUNK, HG_HEADS, HG_DIM).transpose(1, 0, 3, 2, 4)

    causal = jnp.tril(jnp.ones((HG_CHUNK, HG_CHUNK), dtype=bool))[:, :, None]

    def step(S, xs):
        qc, kc, vc, lfc = xs
        b = jnp.cumsum(lfc, axis=2)
        diff = b[:, :, :, None, :] - b[:, :, None, :, :]
        decay = jnp.exp(jnp.where(causal, diff, -jnp.inf))
        A = jnp.einsum('bhtd,bhsd,bhtsd->bhts', qc, kc, decay)
        o = (jnp.einsum('bhts,bhsv->bhtv', A, vc)
             + jnp.einsum('bhtd,bhdv->bhtv', qc * jnp.exp(b), S))
        b_last = b[:, :, -1:, :]
        S = (jnp.exp(b_last[:, :, 0, :])[..., None] * S
             + jnp.einsum('bhsd,bhsv->bhdv', kc * jnp.exp(b_last - b), vc))
        return S, o

    S0 = jnp.zeros((B, HG_HEADS, HG_DIM, HG_DIM), jnp.float32)
    _, o = lax.scan(step, S0, (to_chunks(qf), to_chunks(k), to_chunks(v), to_chunks(log_f)))
    o = o.transpose(1, 0, 3, 2, 4).reshape(B, T, HG_HEADS, HG_DIM)
    o = rms_norm(o, g_norm) * jax.nn.silu(g.astype(jnp.float32).reshape(B, T, HG_HEADS, HG_DIM))
    return o.reshape(B, T, HG_WIDTH).astype(dt)


def compress(a, pe, w1, w2):
    B, T, G, d = a.shape
    R = CMP_LEN // CMP_STRIDE
    n_cmp = (T - CMP_LEN) // CMP_STRIDE + 1
    seg = a.reshape(B, T // CMP_STRIDE, CMP_STRIDE, G, d)
    blocks = jnp.concatenate([seg[:, r:r + n_cmp] for r in range(R)], axis=2)
    blocks = blocks + pe[None, None, :, None, :]
    flat = blocks.transpose(0, 1, 3, 2, 4).reshape(B, n_cmp, G, CMP_LEN * d)
    return jax.nn.silu(flat @ w1) @ w2


def cmp_to_slc_matrix(T):
    n_cmp = (T - CMP_LEN) // CMP_STRIDE + 1
    n_slc = T // SLC_LEN
    c_start = np.arange(n_cmp) * CMP_STRIDE
    c_end = c_start + CMP_LEN - 1
    s_start = np.arange(n_slc) * SLC_LEN
    s_end = s_start + SLC_LEN - 1
    M = (c_start[:, None] <= s_end[None, :]) & (c_end[:, None] >= s_start[None, :])
    return jnp.asarray(M.astype(np.float32))


def nsa_mixer(q, k_cmp, v_cmp, k_slc, v_slc, k_swa, v_swa, gate_logit, q_gain, k_gain,
              pe_k, w1_k, w2_k, pe_v, w1_v, w2_v):
    B, T, _ = q.shape
    dt = q.dtype
    H, G, Hg, d = NSA_HEADS, NSA_KV_HEADS, NSA_GROUP, NSA_DIM
    pos = jnp.arange(T)
    scale = d ** -0.5
    qr = rope(rms_norm(q.reshape(B, T, H, d), q_gain), pos)
    qg = qr.reshape(B, T, G, Hg, d)

    def kv_heads(a):
        return a.reshape(B, T, G, d)

    kc = compress(kv_heads(k_cmp), pe_k, w1_k, w2_k)
    vc = compress(kv_heads(v_cmp), pe_v, w1_v, w2_v)
    n_cmp = kc.shape[1]
    cmp_end = jnp.arange(n_cmp) * CMP_STRIDE + CMP_LEN - 1
    kc = rope(rms_norm(kc, k_gain[0]), cmp_end)
    s = jnp.einsum('btghd,bngd->bghtn', qg, kc).astype(jnp.float32) * scale
    m_cmp = cmp_end[None, :] <= pos[:, None]
    p_cmp = jax.nn.softmax(jnp.where(m_cmp, s, NEG), axis=-1) * m_cmp
    o_cmp = jnp.einsum('bghtn,bngd->btghd', p_cmp.astype(dt), vc)

    n_slc = T // SLC_LEN
    imp = jnp.einsum('bghtn,nj->bgtj', p_cmp, cmp_to_slc_matrix(T))
    blk = jnp.arange(n_slc)
    cur = pos // SLC_LEN
    forced = (blk[None, :] == 0) | (blk[None, :] == cur[:, None]) | (blk[None, :] == cur[:, None] - 1)
    causal_blk = blk[None, :] <= cur[:, None]
    score = jnp.where(forced, FORCED_SCORE, jnp.where(causal_blk, imp, NEG))
    top = min(SLC_TOP, n_slc)
    _, idx = lax.top_k(score, top)

    ks = rope(rms_norm(kv_heads(k_slc), k_gain[1]), pos)
    kb = ks.reshape(B, n_slc, SLC_LEN, G, d).transpose(0, 3, 1, 2, 4)
    vb = kv_heads(v_slc).reshape(B, n_slc, SLC_LEN, G, d).transpose(0, 3, 1, 2, 4)
    nq = T // SLC_QBLOCK
    q_blocks = qg.reshape(B, nq, SLC_QBLOCK, G, Hg, d).transpose(1, 0, 3, 4, 2, 5)
    idx_blocks = idx.reshape(B, G, nq, SLC_QBLOCK, top).transpose(2, 0, 1, 3, 4)
    t_blocks = pos.reshape(nq, SLC_QBLOCK)
    bi = jnp.arange(B)[:, None, None, None]
    gi = jnp.arange(G)[None, :, None, None]
    tok = jnp.arange(SLC_LEN)

    def slc_block(xs):
        qb, ib, tb = xs
        kg = kb[bi, gi, ib]
        vg = vb[bi, gi, ib]
        sb = jnp.einsum('bghqd,bgqnsd->bghqns', qb, kg).astype(jnp.float32) * scale
        kpos = ib[..., None] * SLC_LEN + tok
        mb = (kpos <= tb[None, None, :, None, None])[:, :, None]
        sb = jnp.where(mb, sb, NEG)
        pb = jax.nn.softmax(sb.reshape(B, G, Hg, SLC_QBLOCK, -1), axis=-1).reshape(sb.shape)
        return jnp.einsum('bghqns,bgqnsd->bghqd', pb.astype(dt), vg)

    o_slc = lax.map(slc_block, (q_blocks, idx_blocks, t_blocks))
    o_slc = o_slc.transpose(1, 0, 4, 2, 3, 5).reshape(B, T, G, Hg, d)

    kw = rope(rms_norm(kv_heads(k_swa), k_gain[2]), pos)
    vw = kv_heads(v_swa)
    kpad = jnp.pad(kw, ((0, 0), (WINDOW, 0), (0, 0), (0, 0)))
    vpad = jnp.pad(vw, ((0, 0), (WINDOW, 0), (0, 0), (0, 0)))
    nw = T // SWA_QBLOCK
    span = WINDOW + SWA_QBLOCK
    qw_blocks = qg.reshape(B, nw, SWA_QBLOCK, G, Hg, d).transpose(1, 0, 3, 4, 2, 5)

    def swa_block(xs):
        qb, n = xs
        start = n * SWA_QBLOCK
        kblk = lax.dynamic_slice_in_dim(kpad, start, span, axis=1)
        vblk = lax.dynamic_slice_in_dim(vpad, start, span, axis=1)
        tq = start + jnp.arange(SWA_QBLOCK)
        tk = start - WINDOW + jnp.arange(span)
        mw = (tk[None, :] <= tq[:, None]) & (tk[None, :] > tq[:, None] - WINDOW) & (tk[None, :] >= 0)
        sw = jnp.einsum('bghqd,bkgd->bghqk', qb, kblk).astype(jnp.float32) * scale
        pw = jax.nn.softmax(jnp.where(mw, sw, NEG), axis=-1)
        return jnp.einsum('bghqk,bkgd->bghqd', pw.astype(dt), vblk)

    o_swa = lax.map(swa_block, (qw_blocks, jnp.arange(nw)))
    o_swa = o_swa.transpose(1, 0, 4, 2, 3, 5).reshape(B, T, G, Hg, d)

    gate = jax.nn.sigmoid(gate_logit.astype(jnp.float32)).reshape(B, T, G, Hg, N_BRANCH)
    o = (gate[..., 0:1] * o_cmp.astype(jnp.float32)
         + gate[..., 1:2] * o_slc.astype(jnp.float32)
         + gate[..., 2:3] * o_swa.astype(jnp.float32))
    return o.reshape(B, T, NSA_WIDTH).astype(dt)


def setup_inputs(seed: int = 0) -> dict:
    key = jax.random.key(seed)
    ks = jax.random.split(key, 20)

    def dense(k, shape, fan_in):
        return jax.random.normal(k, shape, jnp.float32) * fan_in ** -0.5

    def gain(k, shape):
        return 1.0 + 0.05 * jax.random.normal(k, shape, jnp.float32)

    return {
        'x': jax.random.normal(ks[0], (BATCH, SEQ, D_MODEL), jnp.float32),
        'w_in': dense(ks[1], (DEPTH, D_MODEL, IN_WIDTH), D_MODEL),
        'w_out': dense(ks[2], (DEPTH, MIX_WIDTH, D_MODEL), MIX_WIDTH),
        'hg_lb_logits': 0.1 * jax.random.normal(ks[3], (DEPTH, HG_WIDTH), jnp.float32),
        'hg_gnorm': gain(ks[4], (DEPTH, HG_DIM)),
        'q_gain': gain(ks[5], (DEPTH, NSA_DIM)),
        'k_gain': gain(ks[6], (DEPTH, N_BRANCH, NSA_DIM)),
        'cmp_pe_k': 0.1 * jax.random.normal(ks[7], (DEPTH, CMP_LEN, NSA_DIM), jnp.float32),
        'cmp_w1_k': dense(ks[8], (DEPTH, CMP_LEN * NSA_DIM, CMP_HIDDEN), CMP_LEN * NSA_DIM),
        'cmp_w2_k': dense(ks[9], (DEPTH, CMP_HIDDEN, NSA_DIM), CMP_HIDDEN),
        'cmp_pe_v': 0.1 * jax.random.normal(ks[10], (DEPTH, CMP_LEN, NSA_DIM), jnp.float32),
        'cmp_w1_v': dense(ks[11], (DEPTH, CMP_LEN * NSA_DIM, CMP_HIDDEN), CMP_LEN * NSA_DIM),
        'cmp_w2_v': dense(ks[12], (DEPTH, CMP_HIDDEN, NSA_DIM), CMP_HIDDEN),
        'w_ffn_in': dense(ks[13], (DEPTH, D_MODEL, 2 * D_FF), D_MODEL),
        'w_ffn_out': dense(ks[14], (DEPTH, D_FF, D_MODEL), D_FF),
        'norm_mix': gain(ks[15], (DEPTH, D_MODEL)),
        'norm_ffn': gain(ks[16], (DEPTH, D_MODEL)),
    }


def reference(x, w_in, w_out, hg_lb_logits, hg_gnorm, q_gain, k_gain,
              cmp_pe_k, cmp_w1_k, cmp_w2_k, cmp_pe_v, cmp_w1_v, cmp_w2_v,
              w_ffn_in, w_ffn_out, norm_mix, norm_ffn):
    lb = jnp.cumsum(jax.nn.softmax(hg_lb_logits.astype(jnp.float32), axis=0), axis=0)
    lb = lb - lb[0:1]
    h = x
    for l in range(DEPTH):
        xn = rms_norm(h, norm_mix[l])
        proj = xn @ w_in[l]
        (hq, hf, hi, hg, nq, kc, vc, ksl, vsl, ksw, vsw, ngate) = jnp.split(proj, IN_SPLIT_POINTS, axis=-1)
        o_hg = hgrn2_mixer(hq, hf, hi, hg, lb[l], hg_gnorm[l])
        o_nsa = nsa_mixer(nq, kc, vc, ksl, vsl, ksw, vsw, ngate, q_gain[l], k_gain[l],
                          cmp_pe_k[l], cmp_w1_k[l], cmp_w2_k[l], cmp_pe_v[l], cmp_w1_v[l], cmp_w2_v[l])
        mix = jnp.concatenate([o_hg, o_nsa], axis=-1)
        h = h + mix @ w_out[l]
        xn = rms_norm(h, norm_ffn[l])
        gu = xn @ w_ffn_in[l]
        g_ff, u_ff = gu[..., :D_FF], gu[..., D_FF:]
        h = h + (jax.nn.silu(g_ff) * u_ff) @ w_ffn_out[l]
    return h
```

```python
import numpy as np
from contextlib import ExitStack
import concourse.bass as bass
import concourse.mybir as mybir
from concourse.bass_utils import run_bass_kernel_spmd

F32 = mybir.dt.float32
BF16 = mybir.dt.bfloat16
AF = mybir.ActivationFunctionType
ALU = mybir.AluOpType

T = 2048
D = 1024
NT = 16
NB = 4
DFF = 2816
NEGM = -30000.0
EPS = 1e-6
NCORES = 8
NSA_STOP = 99
SEQ_PER_CORE = 4


class Res:
    __slots__ = ("name", "w", "rs", "psum")

    def __init__(self, name, after=(), psum=False):
        self.name = name
        self.w = None
        self.rs = []
        self.psum = psum
        for o in after:
            if o.w is not None:
                self.rs.append(o.w)
            self.rs.extend(o.rs)


class Ins:
    __slots__ = ("eng", "fn", "deps", "sig", "dma", "sem", "val", "preds", "succs", "cost", "lat", "idx", "bl", "tag",
                 "start", "crit", "critkind")

    def __init__(self, eng, fn, dma, cost, lat):
        self.eng = eng
        self.fn = fn
        self.deps = []
        self.preds = []
        self.succs = []
        self.sig = False
        self.dma = dma
        self.sem = None
        self.val = None
        self.cost = cost
        self.lat = lat
        self.idx = 0
        self.bl = 0.0


class Prog:
    STREAMS = ("pe", "act", "dve", "pool", "sp")
    KDMA = 8

    def __init__(self, nc):
        self.nc = nc
        self.streams = {s: [] for s in self.STREAMS}
        self.final_waits = []
        self.nwaits = 0
        self.order = []
        self.makespan = 0.0
        self.tag = ""
        self.tagcost = {}
        self.act_prev_group = []
        self.act_group = []
        self.act_first = None

    def op(self, eng, fn, reads=(), writes=(), dma=False, cost=100.0, lat=0.0):
        ins = Ins(eng, fn, dma, cost, lat)
        self.order.append(ins)
        key = (self.tag, eng)
        self.tagcost[key] = self.tagcost.get(key, 0.0) + cost
        ins.tag = self.tag
        deps = []
        for r in reads:
            if r.w is not None:
                deps.append(r.w)
            if r.psum:
                for rr in r.rs:
                    if rr.eng != eng:
                        deps.append(rr)
        for w in writes:
            if w.w is not None:
                deps.append(w.w)
            deps.extend(w.rs)
        seen = set()
        for d in deps:
            if id(d) in seen or d is ins:
                continue
            seen.add(id(d))
            ins.preds.append(d)
            if (not d.dma) and (not dma) and d.eng == "pe" and eng == "pe":
                continue
            ins.deps.append(d)
        for r in reads:
            r.rs.append(ins)
        for w in writes:
            w.w = ins
            w.rs = []
        if eng == "act":
            if self.act_first is None:
                for p_ in self.act_prev_group:
                    if p_ not in ins.preds:
                        ins.preds.append(p_)
                self.act_first = ins
            elif self.act_first not in ins.preds:
                ins.preds.append(self.act_first)
            self.act_group.append(ins)
        self.streams[eng].append(ins)
        return ins

    def act_barrier(self):
        if self.act_group:
            self.act_prev_group = self.act_group
            self.act_group = []
            self.act_first = None

    FIXED = ()

    def schedule(self, fixed=None):
        if fixed is None:
            fixed = self.FIXED
        import heapq
        order = self.order
        n = len(order)
        for i, ins in enumerate(order):
            ins.idx = i
        for s in fixed:
            prev = None
            for ins in self.streams[s]:
                if prev is not None and prev not in ins.preds:
                    ins.preds.append(prev)
                prev = ins
        for ins in order:
            for p in ins.preds:
                p.succs.append(ins)
        for ins in reversed(order):
            m = 0.0
            for sc in ins.succs:
                if sc.bl > m:
                    m = sc.bl
            ins.bl = ins.cost + ins.lat + m
        indeg = [len(ins.preds) for ins in order]
        finish = [0.0] * n
        pending = {s: [] for s in self.STREAMS}
        readyh = {s: [] for s in self.STREAMS}
        eng_free = {s: 0.0 for s in self.STREAMS}

        def push(ins):
            rt = 0.0
            for p in ins.preds:
                t = finish[p.idx] + p.lat + (100.0 if p.eng != ins.eng else (0.0 if ins.eng == "pe" else 40.0))
                if t > rt:
                    rt = t
            heapq.heappush(pending[ins.eng], (rt, -ins.bl, ins.idx))

        for ins in order:
            if indeg[ins.idx] == 0:
                push(ins)
        new_streams = {s: [] for s in self.STREAMS}
        done = 0
        while done < n:
            best = None
            for s in self.STREAMS:
                t = eng_free[s]
                pd = pending[s]
                rd = readyh[s]
                while pd and pd[0][0] <= t:
                    rt, nb, ix = heapq.heappop(pd)
                    heapq.heappush(rd, (nb, ix, rt))
                if rd:
                    cs = t
                elif pd:
                    cs = pd[0][0]
                else:
                    continue
                if best is None or cs < best[0]:
                    best = (cs, s)
            cs, s = best
            if readyh[s]:
                nb, ix, rt = heapq.heappop(readyh[s])
            else:
                rt, nb, ix = heapq.heappop(pending[s])
            ins = order[ix]
            st = max(eng_free[s], rt)
            ins.start = st
            if eng_free[s] >= rt:
                ins.crit = new_streams[s][-1] if new_streams[s] else None
                ins.critkind = "eng"
            else:
                bp, bt = None, -1.0
                for p in ins.preds:
                    t = finish[p.idx] + p.lat + (100.0 if p.eng != ins.eng else (0.0 if ins.eng == "pe" else 40.0))
                    if t > bt:
                        bp, bt = p, t
                ins.crit = bp
                ins.critkind = "dep"
            fin = st + ins.cost
            finish[ix] = fin
            eng_free[s] = fin
            new_streams[s].append(ins)
            done += 1
            for sc in ins.succs:
                indeg[sc.idx] -= 1
                if indeg[sc.idx] == 0:
                    push(sc)
        self.streams = new_streams
        self.makespan = max(finish) if n else 0.0
        self.busy = {s: sum(i.cost for i in new_streams[s]) / 1e3 for s in self.STREAMS}
        last = max(order, key=lambda i: finish[i.idx]) if n else None
        summ = {}
        cur = last
        while cur is not None:
            prev = cur.crit
            t_prev = (finish[prev.idx] if prev is not None else 0.0)
            seg = finish[cur.idx] - t_prev
            key = (cur.tag, cur.eng, cur.critkind + ("<-" + prev.eng + ("*dma" if prev.dma else "") if (prev is not None and cur.critkind == "dep") else ""))
            summ[key] = summ.get(key, 0.0) + seg
            cur = prev
        self.critsum = summ

    def emit(self, sems, do_schedule=True):
        nc = self.nc
        if do_schedule:
            self.schedule()
        for s in self.STREAMS:
            hist = []
            for ins in self.streams[s]:
                if ins.dma:
                    if len(hist) >= self.KDMA and hist[-self.KDMA] not in ins.deps:
                        ins.deps.append(hist[-self.KDMA])
                    hist.append(ins)
        for s in self.STREAMS:
            for ins in self.streams[s]:
                for d in ins.deps:
                    d.sig = True
        for s in self.STREAMS:
            cnt = 0
            nd = 0
            for ins in self.streams[s]:
                if ins.dma:
                    ins.sem = sems["d_%s_%d" % (s, nd % self.KDMA)]
                    ins.val = 16 * (nd // self.KDMA + 1)
                    ins.sig = True
                    nd += 1
                elif ins.sig:
                    cnt += 1
                    ins.sem = sems["c_" + s]
                    ins.val = cnt

        def run(stream, e):
            have = {}
            for ins in self.streams[stream]:
                for d in ins.deps:
                    k = id(d.sem)
                    if have.get(k, 0) >= d.val:
                        continue
                    have[k] = d.val
                    e.wait_ge(d.sem, d.val)
                    self.nwaits += 1
                bi = ins.fn(e)
                if ins.sig:
                    bi.then_inc(ins.sem, 16 if ins.dma else 1)
            if stream == "sp":
                for d in self.final_waits:
                    k = id(d.sem)
                    if have.get(k, 0) >= d.val:
                        continue
                    have[k] = d.val
                    e.wait_ge(d.sem, d.val)

        with nc.Block() as block:
            @block.tensor
            def _(e):
                run("pe", e)

            @block.scalar
            def _(e):
                run("act", e)

            @block.vector
            def _(e):
                run("dve", e)

            @block.gpsimd
            def _(e):
                run("pool", e)

            @block.sync
            def _(e):
                run("sp", e)


def host_consts():
    c = {}
    f32 = np.float32
    c["c_ident"] = np.eye(128, dtype=f32)
    c["c_ones"] = np.ones((128, 128), f32)
    bd = np.zeros((128, 128), f32)
    bd[:64, :64] = 1
    bd[64:, 64:] = 1
    c["c_bd"] = bd
    prot = np.zeros((128, 128), f32)
    for m in range(128):
        mm = m % 64
        base = m - mm
        if mm < 32:
            prot[base + mm + 32, m] = -1.0
        else:
            prot[base + mm - 32, m] = 1.0
    c["c_prot"] = prot
    s = np.arange(128)[:, None]
    t = np.arange(128)[None, :]
    c["c_triA"] = np.where(s <= t, 0.0, NEGM).astype(f32)
    c["c_triB"] = np.where(s > t, 0.0, NEGM).astype(f32)
    same = (s // 64) == (t // 64)
    c["c_hgmask"] = (same & (s <= t)).astype(f32)
    L = np.zeros((128, 132), f32)
    sl = s % 64
    tl = t % 64
    L[:, :128] = np.where(same, (sl <= tl).astype(f32) - (sl <= 31).astype(f32), 0.0)
    sv = np.arange(128)
    L[:, 128] = ((sv < 64) & (sv % 64 <= 31)).astype(f32)
    L[:, 129] = ((sv >= 64) & (sv % 64 <= 31)).astype(f32)
    L[:, 130] = (sv < 64).astype(f32)
    L[:, 131] = (sv >= 64).astype(f32)
    c["c_L"] = L
    c["c_U"] = (same & (s > t)).astype(f32)
    E = np.zeros((128, T), f32)
    for j in range(32):
        E[j, 64 * j:64 * j + 64] = 1.0
    c["c_E"] = E
    n = np.arange(127)[:, None]
    tt = np.arange(T)[None, :]
    cm = np.full((128, T), NEGM, f32)
    cm[:127] = np.where(16 * n + 31 <= tt, 0.0, NEGM)
    c["c_cmpneg"] = cm
    C = np.zeros((128, 8, 32), f32)
    for ii in range(8):
        for tl_ in range(128):
            tq = 128 * (8 + ii) + tl_
            cur = tq // 64
            for j in range(32):
                if j == 0:
                    v = 1e9
                elif j == cur:
                    v = 2e9
                elif j == cur - 1:
                    v = 3e9
                elif j <= cur:
                    v = 0.0
                else:
                    v = -1e30
                C[tl_, ii, j] = v
    c["c_C"] = C
    half = 32
    freqs = (np.float32(10000.0) ** (-np.arange(half, dtype=f32) / np.float32(half))).astype(f32)
    pos = np.arange(T, dtype=f32)
    ang = (pos[:, None] * freqs[None, :]).astype(f32)
    cosv = np.cos(ang).astype(f32)
    sinv = np.sin(ang).astype(f32)
    pidx = (np.arange(128) % 64) % 32
    c["c_cos"] = np.ascontiguousarray(cosv[:, pidx].T)
    c["c_sin"] = np.ascontiguousarray(sinv[:, pidx].T)
    ce = np.arange(127) * 16 + 31
    c["c_cosC"] = np.ascontiguousarray(c["c_cos"][:, ce])
    c["c_sinC"] = np.ascontiguousarray(c["c_sin"][:, ce])
    c_start = np.arange(127) * 16
    c_end = c_start + 31
    s_start = np.arange(32) * 64
    s_end = s_start + 63
    M = ((c_start[:, None] <= s_end[None, :]) & (c_end[:, None] >= s_start[None, :])).astype(f32)
    rt = np.zeros((128, 33), f32)
    rt[:127, 0] = 1.0
    rt[:127, 1:] = M
    c["c_rtail"] = rt
    return c


CONST_SHAPES = {
    "c_ident": [128, 128], "c_ones": [128, 128], "c_bd": [128, 128], "c_prot": [128, 128],
    "c_triA": [128, 128], "c_triB": [128, 128], "c_hgmask": [128, 128], "c_L": [128, 132],
    "c_U": [128, 128], "c_E": [128, T], "c_cmpneg": [128, T], "c_C": [128, 8, 32],
    "c_cos": [128, T], "c_sin": [128, T], "c_cosC": [128, 127], "c_sinC": [128, 127],
    "c_rtail": [128, 33],
}


def host_params(inp):
    p = {}
    f = np.float32
    nm = np.asarray(inp["norm_mix"], f).reshape(2, 8, 128)
    nf = np.asarray(inp["norm_ffn"], f).reshape(2, 8, 128)
    nrm = np.zeros((128, 2, 2, 8), f)
    nrm[:, :, 0, :] = nm.transpose(2, 0, 1)
    nrm[:, :, 1, :] = nf.transpose(2, 0, 1)
    p["p_nrm"] = nrm
    p["p_gn"] = np.ascontiguousarray(np.asarray(inp["hg_gnorm"], f).T)
    p["p_lbl"] = np.ascontiguousarray(np.asarray(inp["hg_lb_logits"], f).reshape(2, 4, 128).transpose(2, 0, 1))
    qg = np.asarray(inp["q_gain"], f)
    p["p_qg"] = np.ascontiguousarray(np.concatenate([qg, qg], axis=1).T)
    kg = np.asarray(inp["k_gain"], f)
    p["p_kg"] = np.ascontiguousarray(np.concatenate([kg, kg], axis=2).transpose(2, 0, 1))
    pk = np.asarray(inp["cmp_pe_k"], f).reshape(2, 16, 128)
    pv = np.asarray(inp["cmp_pe_v"], f).reshape(2, 16, 128)
    pe = np.zeros((128, 2, 2, 16), f)
    pe[:, :, 0, :] = pk.transpose(2, 0, 1)
    pe[:, :, 1, :] = pv.transpose(2, 0, 1)
    p["p_pe"] = pe
    return p


PARAM_SHAPES = {"p_nrm": [128, 2, 2, 8], "p_gn": [128, 2], "p_lbl": [128, 2, 4], "p_qg": [128, 2],
                "p_kg": [128, 2, 3], "p_pe": [128, 2, 2, 16]}

WEIGHT_SHAPES = {"w_in": [2, 1024, 3352], "w_out": [2, 1024, 1024], "cmp_w1_k": [2, 2048, 256],
                 "cmp_w2_k": [2, 256, 64], "cmp_w1_v": [2, 2048, 256], "cmp_w2_v": [2, 256, 64],
                 "w_ffn_in": [2, 1024, 2 * DFF], "w_ffn_out": [2, DFF, 1024]}


def build_program(nseq=SEQ_PER_CORE, nlayers=2, dbg=None, phases=("hg", "nsa", "ffn")):
    nc = bass.Bass("TRN2", target_bir_lowering=False)
    dr = {}
    dr["x"] = nc.dram_tensor("x", [nseq, T, D], F32, kind="ExternalInput").ap()
    for k, sh in WEIGHT_SHAPES.items():
        dr[k] = nc.dram_tensor(k, sh, F32, kind="ExternalInput").ap()
    for k, sh in PARAM_SHAPES.items():
        dr[k] = nc.dram_tensor(k, sh, F32, kind="ExternalInput").ap()
    for k, sh in CONST_SHAPES.items():
        dr[k] = nc.dram_tensor(k, sh, F32, kind="ExternalInput").ap()
    out_d = nc.dram_tensor("out", [nseq, T, D], F32, kind="ExternalOutput").ap()
    dbg_d = {}
    if dbg:
        for name, (shape, dt) in dbg.items():
            dbg_d[name] = nc.dram_tensor("dbg_" + name, shape, dt, kind="ExternalOutput").ap()

    es = ExitStack()
    with es:
        def sb(name, shape, dt):
            return es.enter_context(nc.sbuf_tensor("s_" + name, shape, dt))

        P = Prog(nc)
        sems = {}
        for s in Prog.STREAMS:
            sems["c_" + s] = es.enter_context(nc.semaphore("c_" + s))
            for k in range(Prog.KDMA):
                sems["d_%s_%d" % (s, k)] = es.enter_context(nc.semaphore("d_%s_%d" % (s, k)))

        hT = sb("hT", [128, 8, T], F32)
        xn = sb("xn", [128, 8, T], BF16)
        ARB = 78 * 1024
        AR = sb("arena", [128, ARB // 2], BF16)
        TMP0 = 16 * 1024
        WB = [sb("wb0", [128, 4096], BF16), sb("wb1", [128, 4096], BF16)]
        RWB = [Res("wb0"), Res("wb1")]
        PS = [es.enter_context(nc.psum_tensor("ps%d" % k, [128, 512], F32)) for k in range(8)]
        RPS = [Res("ps%d" % k, psum=True) for k in range(8)]

        cb = {}
        for k in ("c_ident", "c_ones", "c_bd", "c_prot", "c_triA", "c_triB", "c_hgmask", "c_rtail"):
            cb[k] = sb(k, CONST_SHAPES[k], BF16)
        cb["c_E"] = sb("c_E", [128, T], BF16)
        cb["c_cmpneg"] = sb("c_cmpneg", [128, T], BF16)
        for k in ("c_L", "c_U", "c_C", "c_cosC", "c_sinC"):
            cb[k] = sb(k, CONST_SHAPES[k], F32)
        cb["c_identf"] = sb("c_identf", [128, 128], F32)
        pm = {}
        for k, sh in PARAM_SHAPES.items():
            pm[k] = sb(k, sh, F32)
        peb = sb("peb", [128, 2, 2, 16], BF16)
        lbF = sb("lbF", [128, 2, 4], F32)
        omlbF = sb("omlbF", [128, 2, 4], F32)
        lbt = sb("lbt", [128, 4], F32)
        cstT = sb("cstT", [128, 2, 2, 2], F32)
        ncstT = sb("ncstT", [128, 2, 2, 2], F32)
        Rcon = Res("consts")
        Rcst = Res("cst")

        arena_regs = []

        def carve(off, dt, n):
            if dt == F32:
                return AR[:, off // 2: off // 2 + 2 * n].bitcast(F32)
            return AR[:, off // 2: off // 2 + n]

        def ares(name, off, nbytes):
            st, en = off, off + nbytes
            old = [r for (s_, e_, r) in arena_regs if s_ < en and st < e_]
            res = Res(name, after=old)
            keep = []
            for (s_, e_, r) in arena_regs:
                if s_ < en and st < e_:
                    if s_ < st:
                        keep.append((s_, st, r))
                    if en < e_:
                        keep.append((en, e_, r))
                else:
                    keep.append((s_, e_, r))
            arena_regs[:] = keep
            arena_regs.append((st, en, res))
            return res

        def rereg(st, en, res):
            keep = []
            for (s_, e_, r) in arena_regs:
                if s_ < en and st < e_:
                    if s_ < st:
                        keep.append((s_, st, r))
                    if en < e_:
                        keep.append((en, e_, r))
                else:
                    keep.append((s_, e_, r))
            arena_regs[:] = keep
            arena_regs.append((st, en, res))

        def alloc(name, off, dt, n):
            nb = n * (4 if dt == F32 else 2)
            assert off + nb <= ARB, (name, off, nb)
            return carve(off, dt, n), ares(name, off, nb)

        def ecost(eng, n):
            if eng == "act":
                return 190.0 + 0.83 * n
            if eng == "pool":
                return 160.0 + 1.7 * n
            return 70.0 + 1.0 * n

        def MM(out, lhsT, rhs, st, sp, R, W, skip=False):
            n = rhs.free_size()
            c = (100.0 + 0.3 * max(64, n)) * (2.5 if rhs.dtype == F32 else 1.0)
            P.op("pe", lambda e: e.matmul(out, lhsT=lhsT, rhs=rhs, start=st, stop=sp, skip_group_check=skip), R, W, cost=c)

        def TR(out, in_, ident, R, W):
            c = 280.0 if in_.dtype == F32 else 90.0
            P.op("pe", lambda e: e.transpose(out, in_, ident), R, W, cost=c)

        def ACT(out, in_, func, R, W, scale=1.0, bias=0.0):
            P.op("act", lambda e: e.activation(out=out, in_=in_, func=func, scale=scale, bias=bias), R, W,
                 cost=ecost("act", in_.free_size()))

        def TT(eng, out, in0, in1, op, R, W):
            P.op(eng, lambda e: e.tensor_tensor(out=out, in0=in0, in1=in1, op=op), R, W, cost=ecost(eng, in0.free_size()))

        def TS(eng, out, in0, s1, s2, op0, op1, R, W):
            c = ecost(eng, in0.free_size())
            if s2 is None:
                P.op(eng, lambda e: e.tensor_scalar(out=out, in0=in0, scalar1=s1, scalar2=None, op0=op0), R, W, cost=c)
            else:
                P.op(eng, lambda e: e.tensor_scalar(out=out, in0=in0, scalar1=s1, scalar2=s2, op0=op0, op1=op1), R, W, cost=c)

        def STT(out, in0, scalar, in1, op0, op1, R, W):
            P.op("dve", lambda e: e.scalar_tensor_tensor(out=out, in0=in0, scalar=scalar, in1=in1, op0=op0, op1=op1), R, W,
                 cost=ecost("dve", in0.free_size()))

        def CP(eng, out, in_, R, W):
            c = ecost(eng, in_.free_size())
            if eng == "act":
                P.op("act", lambda e: e.copy(out=out, in_=in_), R, W, cost=c)
            else:
                P.op(eng, lambda e: e.tensor_copy(out=out, in_=in_), R, W, cost=c)

        def RECIP(out, in_, R, W):
            P.op("dve", lambda e: e.reciprocal(out=out, in_=in_), R, W, cost=ecost("dve", in_.free_size()))

        def MEMSET(eng, ap, val, W):
            P.op(eng, lambda e: e.memset(ap, val), (), W, cost=ecost(eng, ap.free_size()))

        def DMA(eng, out, in_, R, W, slow=False):
            lat = 2000.0 + max(out.nbytes(), in_.nbytes()) / 180.0
            if slow:
                return P.op(eng, lambda e: e.dma_start(out=out, in_=in_, allow_slow_non_contiguous=True), R, W, dma=True,
                            cost=60.0, lat=lat)
            return P.op(eng, lambda e: e.dma_start(out=out, in_=in_), R, W, dma=True, cost=60.0, lat=lat)

        Rdram = Res("dram_out")
        out_dmas = []

        def dump(name, ap, R):
            if dbg and name in dbg_d:
                out_dmas.append(DMA("sp", dbg_d[name], ap, R, [Rdram]))
                del dbg_d[name]

        psrr = [0]

        def bank():
            k = psrr[0] % 6
            psrr[0] += 1
            return PS[k], RPS[k]

        resrr = [0]

        def rbank():
            k = 6 + resrr[0] % 2
            resrr[0] += 1
            return PS[k], RPS[k]

        for k in ("c_ident", "c_ones", "c_bd", "c_prot", "c_triA", "c_triB", "c_hgmask", "c_rtail", "c_cmpneg"):
            DMA("pool", cb[k][:], dr[k], [], [Rcon])
        DMA("pool", cb["c_E"][:], dr["c_E"], [], [Rcon])
        for k in ("c_L", "c_U", "c_C", "c_cosC", "c_sinC"):
            DMA("sp", cb[k][:], dr[k], [], [Rcon])
        DMA("sp", cb["c_identf"][:], dr["c_ident"], [], [Rcon])
        for k in PARAM_SHAPES:
            DMA("sp", pm[k][:], dr[k], [], [Rcon])
        DMA("pool", peb[:], dr["p_pe"], [], [Rcon])
        ident_b = cb["c_ident"]
        ident_f = cb["c_identf"]
        ones_b = cb["c_ones"]
        TT("dve", lbt[:], pm["p_lbl"][:, 1, :], pm["p_lbl"][:, 0, :], ALU.subtract, [Rcon], [Rcon])
        ACT(lbt[:], lbt[:], AF.Exp, [Rcon], [Rcon], scale=-1.0)
        TS("dve", lbt[:], lbt[:], 1.0, None, ALU.add, None, [Rcon], [Rcon])
        RECIP(lbF[:, 1, :], lbt[:], [Rcon], [Rcon])
        MEMSET("dve", lbF[:, 0, :], 0.0, [Rcon])
        TS("dve", omlbF[:], lbF[:], -1.0, 1.0, ALU.mult, ALU.add, [Rcon], [Rcon])

        RhT = [[Res("hT%d_%d" % (c, b)) for b in range(NB)] for c in range(8)]
        Rxn = [[Res("xn%d_%d" % (c, b)) for b in range(NB)] for c in range(8)]

        def bsl(b):
            return slice(512 * b, 512 * b + 512)

        def tsl(i):
            return slice(128 * i, 128 * i + 128)

        wrr = [0]

        wsubs = [[], [], [], []]

        def _claim(minis):
            olds = []
            for m_ in minis:
                olds.extend(wsubs[m_])
            base = Res("wbase", after=olds)
            for m_ in minis:
                wsubs[m_] = []

            def newsub():
                r_ = Res("wsub", after=[base])
                for m_ in minis:
                    wsubs[m_].append(r_)
                return r_
            return newsub

        def wslot():
            k = wrr[0] % 2
            wrr[0] += 1
            return WB[k], _claim([2 * k, 2 * k + 1])

        hrr = [0]

        def whalf():
            m_ = hrr[0] % 4
            hrr[0] += 1
            return WB[m_ // 2][:, 2048 * (m_ % 2):2048 * (m_ % 2) + 2048], _claim([m_])

        def load_x(s):
            P.tag = "load"
            for i in range(NT):
                xin, Rxin = alloc("xin%d" % (i % 4), TMP0 + (i % 4) * 4096, F32, 1024)
                DMA("sp", xin, dr["x"][s, tsl(i), :], [], [Rxin])
                for half in range(2):
                    pb, Rpb = bank()
                    for cc in range(4):
                        c = half * 4 + cc
                        TR(pb[:, 128 * cc:128 * cc + 128], xin[:, 128 * c:128 * c + 128], ident_f[:], [Rxin, Rcon], [Rpb])
                    dst = hT[:, half * 4:half * 4 + 4, tsl(i)]
                    src = pb[:, :].rearrange("p (c t) -> p c t", c=4)
                    b = i // 4
                    W = [RhT[half * 4 + cc][b] for cc in range(4)]
                    if half == 0:
                        CP("act", dst, src, [Rpb], W)
                    else:
                        CP("dve", dst, src, [Rpb], W)

        def store_out(s):
            P.tag = "store"
            for i in range(NT):
                xo, Rxo = alloc("xo%d" % (i % 4), TMP0 + (i % 4) * 4096, F32, 1024)
                b = i // 4
                for half in range(2):
                    pb, Rpb = bank()
                    for cc in range(4):
                        c = half * 4 + cc
                        TR(pb[:, 128 * cc:128 * cc + 128], hT[:, c, tsl(i)], ident_f[:], [RhT[c][b], Rcon], [Rpb])
                    if half == 0:
                        CP("act", xo[:, 0:512], pb[:, :], [Rpb], [Rxo])
                    else:
                        CP("dve", xo[:, 512:1024], pb[:, :], [Rpb], [Rxo])
                out_dmas.append(DMA("sp", out_d[s, tsl(i), :], xo, [Rxo], [Rdram]))

        def rmsnorm(l, which, toff):
            P.tag = "norm"
            for b in range(NB):
                sq, Rsq = alloc("nsq", toff, BF16, 8 * 512)
                rs, Rrs = alloc("nrs", toff + 8192, F32, 512)
                pb, Rpb = bank()
                for c in range(8):
                    ACT(sq[:, 512 * c:512 * c + 512], hT[:, c, bsl(b)], AF.Square, [RhT[c][b]], [Rsq])
                for c in range(8):
                    MM(pb[:, :], ones_b[:], sq[:, 512 * c:512 * c + 512], c == 0, c == 7, [Rsq, Rcon], [Rpb])
                ACT(rs, pb[:, :], AF.Ln, [Rpb], [Rrs], scale=1.0 / D, bias=EPS)
                ACT(rs, rs, AF.Exp, [Rrs], [Rrs], scale=-0.5)
                for c in range(8):
                    STT(xn[:, c, bsl(b)], hT[:, c, bsl(b)], pm["p_nrm"][:, l, which, c:c + 1], rs,
                        ALU.mult, ALU.mult, [RhT[c][b], Rrs, Rcon], [Rxn[c][b]])

        def hgrn2(l, Rmix):
            P.tag = "hg_proj"
            win = dr["w_in"][l]
            wv = win[:, 0:2048].rearrange("(kc p) (j hh n) -> p kc j hh n", p=128, j=4, hh=4)
            for h in range(4):
                P.tag = "hg_proj"
                wt, nsub = wslot()
                wtv = wt[:, :].rearrange("p (kc j n) -> p kc j n", kc=8, j=4)
                Rwj = [nsub() for _ in range(4)]
                for j_ in range(4):
                    DMA("pool", wtv[:, :, j_, :], wv[:, :, j_, h, :], [], [Rwj[j_]])
                o = TMP0
                def blocked(name, off, dt):
                    esz = 4 if dt == F32 else 2
                    aps, rss = [], []
                    for b_ in range(NB):
                        a_, r_ = alloc("%s_%d" % (name, b_), off + 512 * esz * b_, dt, 512)
                        aps.append(a_)
                        rss.append(r_)
                    return carve(off, dt, T), rss
                qs, RqsB = blocked("qs", o, BF16)
                gs, RgsB = blocked("gs", o + 4096, BF16)
                vt, RvtB = blocked("vt", o + 8192, BF16)
                lfT, RlfTB = blocked("lfT", o + 12288, F32)
                kTf, RkTfB = blocked("kTf", o + 20480, F32)
                o2 = o + 28672
                sgts = []
                for k_ in range(3):
                    a_, r_ = alloc("sgt%d" % k_, o2, F32, 512)
                    o2 += 2048
                    sgts.append((a_, r_))
                for b in range(NB):
                    sgt, Rsgt = sgts[0]
                    pb, Rpb = bank()
                    for kc in range(8):
                        MM(pb[:, :], wtv[:, kc, 0, :], xn[:, kc, bsl(b)], kc == 0, kc == 7, [Rwj[0], Rxn[kc][b]], [Rpb])
                    ACT(sgt, pb[:, :], AF.Exp, [Rpb], [Rsgt], scale=-1.0)
                    ACT(sgt, sgt, AF.Ln, [Rsgt], [Rsgt], bias=1.0)
                    ACT(sgt, sgt, AF.Exp, [Rsgt], [Rsgt], scale=-1.0)
                    TT("dve", qs[:, bsl(b)], pb[:, :], sgt, ALU.mult, [Rpb, Rsgt], [RqsB[b]])
                    sgt, Rsgt = sgts[1]
                    pb, Rpb = bank()
                    for kc in range(8):
                        MM(pb[:, :], wtv[:, kc, 3, :], xn[:, kc, bsl(b)], kc == 0, kc == 7, [Rwj[3], Rxn[kc][b]], [Rpb])
                    ACT(sgt, pb[:, :], AF.Exp, [Rpb], [Rsgt], scale=-1.0)
                    ACT(sgt, sgt, AF.Ln, [Rsgt], [Rsgt], bias=1.0)
                    ACT(sgt, sgt, AF.Exp, [Rsgt], [Rsgt], scale=-1.0)
                    TT("dve", gs[:, bsl(b)], pb[:, :], sgt, ALU.mult, [Rpb, Rsgt], [RgsB[b]])
                    sgt, Rsgt = sgts[2]
                    pb, Rpb = bank()
                    for kc in range(8):
                        MM(pb[:, :], wtv[:, kc, 1, :], xn[:, kc, bsl(b)], kc == 0, kc == 7, [Rwj[1], Rxn[kc][b]], [Rpb])
                    ACT(sgt, pb[:, :], AF.Exp, [Rpb], [Rsgt])
                    ACT(sgt, sgt, AF.Ln, [Rsgt], [Rsgt], bias=1.0)
                    ACT(sgt, sgt, AF.Exp, [Rsgt], [Rsgt], scale=-1.0)
                    TS("dve", kTf[:, bsl(b)], sgt, omlbF[:, l, h:h + 1], None, ALU.mult, None, [Rsgt, Rcon], [RkTfB[b]])
                    TS("dve", lfT[:, bsl(b)], kTf[:, bsl(b)], -1.0, 1.0, ALU.mult, ALU.add, [RkTfB[b]], [RlfTB[b]])
                    pb, Rpb = bank()
                    for ti in range(4):
                        i = 4 * b + ti
                        for kc in range(8):
                            MM(pb[:, 128 * ti:128 * ti + 128], xn[:, kc, tsl(i)], wtv[:, kc, 2, :], kc == 0, kc == 7,
                               [Rwj[2], Rxn[kc][b]], [Rpb])
                    CP("act", vt[:, bsl(b)], pb[:, :], [Rpb], [RvtB[b]])
                P.tag = "hg_rec"
                for b in range(NB):
                    ACT(lfT[:, bsl(b)], lfT[:, bsl(b)], AF.Ln, [RlfTB[b]], [RlfTB[b]])
                S, RS = alloc("S", o2, F32, 128)
                o2 += 512
                MEMSET("dve", S, 0.0, [RS])
                Sp = []
                for k in range(2):
                    a_, r_ = alloc("Sp%d" % k, o2, BF16, 128)
                    o2 += 256
                    Sp.append((a_, r_))
                tl_bufs = []
                for k in range(2):
                    d_ = {}
                    for nm, dt_, n_ in (("lft", F32, 128), ("eq", F32, 128), ("ek", F32, 128), ("ebr", F32, 128),
                                        ("qt", BF16, 128), ("kt", BF16, 128), ("kh", BF16, 128), ("AT", BF16, 128),
                                        ("eml", F32, 4)):
                        a_, r_ = alloc("%s%d" % (nm, k), o2, dt_, n_)
                        o2 += n_ * (4 if dt_ == F32 else 2)
                        d_[nm] = (a_, r_)
                    tl_bufs.append(d_)
                osq, Rosq = alloc("osq", o2, BF16, 512)
                o2 += 1024
                orr, Rorr = alloc("orr", o2, F32, 512)
                o2 += 2048
                otm, Rotm = alloc("otm", o2, F32, 512)
                o2 += 2048
                pbo = None
                for i in range(NT):
                    tb = tl_bufs[i % 2]
                    lft, Rlft = tb["lft"]
                    eq, Req = tb["eq"]
                    ek, Rek = tb["ek"]
                    ebr, Rebr = tb["ebr"]
                    qt, Rqt = tb["qt"]
                    kt, Rkt = tb["kt"]
                    kh, Rkh = tb["kh"]
                    AT_, RAT = tb["AT"]
                    eml, Reml = tb["eml"]
                    b = i // 4
                    Rqs, Rgs, Rvt, RlfT, RkTf = RqsB[b], RgsB[b], RvtB[b], RlfTB[b], RkTfB[b]
                    ptr, Rptr = bank()
                    TR(ptr[:, 0:128], lfT[:, tsl(i)], ident_f[:], [RlfT, Rcon], [Rptr])
                    TR(ptr[:, 128:256], kTf[:, tsl(i)], ident_f[:], [RkTf, Rcon], [Rptr])
                    CP("act", lft, ptr[:, 0:128], [Rptr], [Rlft])
                    pcs, Rpcs = bank()
                    MM(pcs[:, 0:132], lft, cb["c_L"][:], True, True, [Rlft, Rcon], [Rpcs])
                    MM(pcs[:, 256:384], cb["c_U"][:], lft, True, True, [Rlft, Rcon], [Rpcs], skip=True)
                    ACT(eq, pcs[:, 0:128], AF.Exp, [Rpcs], [Req])
                    ACT(ek, pcs[:, 0:128], AF.Exp, [Rpcs], [Rek], scale=-1.0)
                    ACT(eml, pcs[:, 128:132], AF.Exp, [Rpcs], [Reml])
                    ACT(ebr, pcs[:, 256:384], AF.Exp, [Rpcs], [Rebr])
                    TT("dve", qt, qs[:, tsl(i)], eq, ALU.mult, [Rqs, Req], [Rqt])
                    TT("dve", kt, kTf[:, tsl(i)], ek, ALU.mult, [RkTf, Rek], [Rkt])
                    TT("dve", kh, ptr[:, 128:256], ebr, ALU.mult, [Rptr, Rebr], [Rkh])
                    pa, Rpa = bank()
                    MM(pa[:, 0:128], kt, qt, True, True, [Rkt, Rqt], [Rpa])
                    TT("dve", AT_, pa[:, 0:128], cb["c_hgmask"][:], ALU.mult, [Rpa, Rcon], [RAT])
                    if i % 4 == 0:
                        pbo, Rpbo = rbank()
                    oc = pbo[:, 128 * (i % 4):128 * (i % 4) + 128]
                    MM(oc, vt[:, tsl(i)], AT_, True, False, [Rvt, RAT], [Rpbo])
                    for ch in range(2):
                        spa, Rspa = Sp[ch]
                        TS("dve", spa, S, eml[:, ch:ch + 1], None, ALU.mult, None, [RS, Reml], [Rspa])
                        MM(oc[:, 64 * ch:64 * ch + 64], spa, qt[:, 64 * ch:64 * ch + 64], False, ch == 1,
                           [Rspa, Rqt], [Rpbo])
                        pkv, Rpkv = bank()
                        MM(pkv[:, 0:128], kh[64 * ch:64 * ch + 64, :], vt[64 * ch:64 * ch + 64, tsl(i)], True, True,
                           [Rkh, Rvt], [Rpkv])
                        STT(S, S, eml[:, 2 + ch:3 + ch], pkv[:, 0:128], ALU.mult, ALU.add, [RS, Reml, Rpkv], [RS])
                    if i % 4 == 3:
                        ACT(osq, pbo[:, :], AF.Square, [Rpbo], [Rosq])
                        pss, Rpss = bank()
                        MM(pss[:, :], ones_b[:], osq, True, True, [Rosq, Rcon], [Rpss])
                        ACT(orr, pss[:, :], AF.Ln, [Rpss], [Rorr], scale=1.0 / 128, bias=EPS)
                        ACT(orr, orr, AF.Exp, [Rorr], [Rorr], scale=-0.5)
                        TT("dve", otm, pbo[:, :], orr, ALU.mult, [Rpbo, Rorr], [Rotm])
                        STT(AR[:, h * T + 512 * b:h * T + 512 * b + 512], otm, pm["p_gn"][:, l:l + 1], gs[:, bsl(b)],
                            ALU.mult, ALU.mult, [Rotm, Rcon, Rgs], [Rmix[h][b]])
                if dbg and h == 0:
                    dump("hg_qs", qs, RqsB)
                    dump("hg_lfT", lfT, RlfTB)

        def outproj(l, part, Rmix):
            P.tag = "outproj"
            wo = dr["w_out"][l]
            for cp in range(4):
                wt, nsub = wslot()
                Rw = nsub()
                wtv = wt[:, 0:1024].rearrange("p (kc n) -> p kc n", kc=4)
                src = wo[512 * part:512 * part + 512, 256 * cp:256 * cp + 256].rearrange("(kc p) n -> p kc n", p=128)
                DMA("pool", wtv, src, [], [Rw])
                for cc in range(2):
                    co = 2 * cp + cc
                    for b in range(NB):
                        pb, Rpb = bank()
                        for kc in range(4):
                            MM(pb[:, :], wtv[:, kc, 128 * cc:128 * cc + 128], AR[:, kc * T + 512 * b:kc * T + 512 * b + 512],
                               kc == 0, kc == 3, [Rw, Rmix[kc][b]], [Rpb])
                        TT("dve", hT[:, co, bsl(b)], pb[:, :], hT[:, co, bsl(b)], ALU.add, [Rpb, RhT[co][b]], [RhT[co][b]])

        def nsa(l, Rmix):
            P.tag = "nsa_n1"
            win = dr["w_in"][l]

            def wcols(c0, n):
                return win[:, c0:c0 + n].rearrange("(kc p) n -> p kc n", p=128)

            for g in range(2):
                P.tag = "nsa_n1"
                o = TMP0
                qT, RqT = alloc("qT", o, BF16, 4 * T)
                kTs, RkTs = alloc("kTs", o + 16384, BF16, T)
                kTw, RkTw = alloc("kTw", o + 20480, BF16, T)
                vS, RvS = alloc("vS", o + 24576, BF16, 16 * 65)
                vW, RvW = alloc("vW", o + 24576 + 2080, BF16, 16 * 65)
                gate, Rgate = alloc("gate", o + 24576 + 4160, F32, 16 * 12)
                kcT, RkcT = alloc("kcT", o + 24576 + 4160 + 768, BF16, 128)
                Rt, RRt = alloc("Rt", o + 24576 + 4160 + 768 + 256, BF16, 98)
                o = TMP0 + 30 * 1024
                qTv = qT.rearrange("p (i h t) -> p i h t", i=NT, h=4)
                vSv = vS.rearrange("p (i d) -> p i d", i=16)
                vWv = vW.rearrange("p (i d) -> p i d", i=16)
                gatev = gate.rearrange("p (i k) -> p i k", i=16)
                MEMSET("pool", vSv[:, :, 64:65], 1.0, [RvS])
                MEMSET("pool", vWv[:, :, 64:65], 1.0, [RvW])
                CP("pool", Rt[:, 64:97], cb["c_rtail"][:, :], [Rcon], [RRt])
                MEMSET("pool", qT[64:128, :], 0.0, [RqT])
                MEMSET("pool", kTs[64:128, :], 0.0, [RkTs])
                DMA("pool", kTs[64:96, :], dr["c_E"][0:32, :], [], [RkTs])
                RqU = [Res("qU%d" % i_, after=[RqT]) for i_ in range(NT)]
                MEMSET("pool", kTw[64:128, :], 0.0, [RkTw])
                MEMSET("pool", kcT[64:128, :], 0.0, [RkcT])
                rope_b = []
                for k in range(2):
                    a1, r1 = alloc("cosb%d" % k, o, F32, 512)
                    a2, r2 = alloc("sinb%d" % k, o + 2048, F32, 512)
                    o += 4096
                    rope_b.append((a1, r1, a2, r2))
                A2, RA2 = alloc("A2", o, BF16, T)
                o += 4096
                hs, Rhs = alloc("hs", o, BF16, 256)
                o += 512
                sets = []
                for k in range(2):
                    d_ = {}
                    for nm, dt_, n_ in (("sq", BF16, 512), ("lnr", F32, 512), ("qn", F32, 512), ("qnb", BF16, 512),
                                        ("t1", F32, 512)):
                        a_, r_ = alloc("%s%d" % (nm, k), o, dt_, n_)
                        o += n_ * (4 if dt_ == F32 else 2)
                        d_[nm] = (a_, r_)
                    sets.append(d_)
                setrr = [0]

                def norm_rope(pb, Rpb, gain_ap, cosap, sinap, Rrope, outs, Rout, n):
                    st_ = sets[setrr[0] % 2]
                    setrr[0] += 1
                    sq, Rsq = st_["sq"]
                    lnr, Rlnr = st_["lnr"]
                    qn, Rqn = st_["qn"]
                    qnb, Rqnb = st_["qnb"]
                    t1, Rt1 = st_["t1"]
                    ACT(sq[:, 0:n], pb[:, 0:n], AF.Square, [Rpb], [Rsq])
                    p2, Rp2 = bank()
                    MM(p2[:, 0:n], cb["c_bd"][:], sq[:, 0:n], True, True, [Rsq, Rcon], [Rp2])
                    ACT(lnr[:, 0:n], p2[:, 0:n], AF.Ln, [Rp2], [Rlnr], scale=1.0 / 64, bias=EPS)
                    ACT(lnr[:, 0:n], lnr[:, 0:n], AF.Exp, [Rlnr], [Rlnr], scale=-0.5)
                    STT(qnb[:, 0:n], pb[:, 0:n], gain_ap, lnr[:, 0:n], ALU.mult, ALU.mult, [Rpb, Rcon, Rlnr], [Rqnb])
                    p3, Rp3 = bank()
                    MM(p3[:, 0:n], cb["c_prot"][:], qnb[:, 0:n], True, True, [Rqnb, Rcon], [Rp3])
                    TT("pool", t1[:, 0:n], qnb[:, 0:n], cosap, ALU.mult, [Rqnb] + Rrope, [Rt1])
                    TT("dve", qn[:, 0:n], p3[:, 0:n], sinap, ALU.mult, [Rp3] + Rrope, [Rqn])
                    for (oap, psl) in outs:
                        if len(oap.shape) == 3:
                            TT("dve", oap, qn[psl, 0:n].rearrange("p (a t) -> p a t", a=oap.shape[1]),
                               t1[psl, 0:n].rearrange("p (a t) -> p a t", a=oap.shape[1]), ALU.add, [Rqn, Rt1], [Rout])
                        else:
                            TT("dve", oap, qn[psl, 0:n], t1[psl, 0:n], ALU.add, [Rqn, Rt1], [Rout])

                wA, nsubA = wslot()
                wAv = wA[:, :].rearrange("p (kc n) -> p kc n", kc=8)
                RwA = [nsubA() for _ in range(5)]
                DMA("pool", wAv[:, :, 0:256], wcols(2048 + 256 * g, 256), [], [RwA[0]])
                for dup in range(2):
                    DMA("pool", wAv[:, :, 256 + 64 * dup:320 + 64 * dup], wcols(2816 + 64 * g, 64), [], [RwA[1 + dup]])
                    DMA("pool", wAv[:, :, 384 + 64 * dup:448 + 64 * dup], wcols(3072 + 64 * g, 64), [], [RwA[3 + dup]])
                wB, nsubB = wslot()
                wBv = wB[:, :].rearrange("p (kc n) -> p kc n", kc=8)
                RwB = [nsubB() for _ in range(7)]
                DMA("pool", wBv[:, :, 0:64], wcols(2944 + 64 * g, 64), [], [RwB[0]])
                DMA("pool", wBv[:, :, 64:128], wcols(3200 + 64 * g, 64), [], [RwB[1]])
                DMA("pool", wBv[:, :, 128:140], wcols(3328 + 12 * g, 12), [], [RwB[2]])
                for dup in range(2):
                    DMA("pool", wBv[:, :, 192 + 64 * dup:256 + 64 * dup], wcols(2560 + 64 * g, 64), [], [RwB[3 + dup]])
                    DMA("pool", wBv[:, :, 320 + 64 * dup:384 + 64 * dup], wcols(2688 + 64 * g, 64), [], [RwB[5 + dup]])

                for b in range(NB):
                    cosb, Rcosb, sinb, Rsinb = rope_b[b % 2]
                    DMA("sp", cosb, dr["c_cos"][:, bsl(b)], [], [Rcosb])
                    DMA("sp", sinb, dr["c_sin"][:, bsl(b)], [], [Rsinb])
                    lo, hi, al = slice(0, 64), slice(64, 128), slice(0, 128)
                    for (c0, gain_ap, dst, Rdst) in (
                            (0, pm["p_qg"][:, l:l + 1], [(qTv[0:64, 4 * b:4 * b + 4, 0, :], lo), (qTv[0:64, 4 * b:4 * b + 4, 1, :], hi)], RqT),
                            (128, pm["p_qg"][:, l:l + 1], [(qTv[0:64, 4 * b:4 * b + 4, 2, :], lo), (qTv[0:64, 4 * b:4 * b + 4, 3, :], hi)], RqT),
                            (256, pm["p_kg"][:, l, 1:2], [(kTs[0:64, bsl(b)], lo)], RkTs),
                            (384, pm["p_kg"][:, l, 2:3], [(kTw[0:64, bsl(b)], lo)], RkTw)):
                        pb, Rpb = bank()
                        for kc in range(8):
                            MM(pb[:, :], wAv[:, kc, c0:c0 + 128], xn[:, kc, bsl(b)], kc == 0, kc == 7,
                               RwA + [Rxn[kc][b]], [Rpb])
                        norm_rope(pb, Rpb, gain_ap, cosb, sinb, [Rcosb, Rsinb], dst, Rdst, 512)
                if NSA_STOP <= 1:
                    continue
                pgate, Rpgate = None, None
                for i0 in range(0, NT, 3):
                    nt_ = min(3, NT - i0)
                    pb, Rpb = bank()
                    for ti in range(nt_):
                        i = i0 + ti
                        for kc in range(8):
                            MM(pb[:, 140 * ti:140 * ti + 140], xn[:, kc, tsl(i)], wBv[:, kc, 0:140], kc == 0, kc == 7,
                               RwB[0:3] + [Rxn[kc][i // 4]], [Rpb], skip=True)
                    pbv = pb[:, 0:140 * nt_].rearrange("p (i d) -> p i d", i=nt_)
                    CP("act", vSv[:, i0:i0 + nt_, 0:64], pbv[:, :, 0:64], [Rpb], [RvS])
                    CP("dve", vWv[:, i0:i0 + nt_, 0:64], pbv[:, :, 64:128], [Rpb], [RvW])
                    ACT(gatev[:, i0:i0 + nt_, :], pbv[:, :, 128:140], AF.Exp, [Rpb], [Rgate], scale=-1.0)
                TS("dve", gate, gate, 1.0, None, ALU.add, None, [Rgate], [Rgate])
                RECIP(gate, gate, [Rgate], [Rgate])
                if NSA_STOP <= 2:
                    continue
                for kv in range(2):
                    c0 = 192 if kv == 0 else 320
                    for b in range(NB):
                        pb, Rpb = bank()
                        for kc in range(8):
                            MM(pb[:, :], wBv[:, kc, c0:c0 + 128], xn[:, kc, bsl(b)], kc == 0, kc == 7,
                               RwB[3:7] + [Rxn[kc][b]], [Rpb])
                        CP("act", A2[0:64, bsl(b)], pb[0:64, :], [Rpb], [RA2])
                        if b == 0:
                            CP("dve", A2[64:128, 0:511], pb[64:128, 1:512], [Rpb], [RA2])
                        else:
                            CP("dve", A2[64:128, 512 * b - 1:512 * b + 511], pb[64:128, :], [Rpb], [RA2])
                    w1 = dr["cmp_w1_k" if kv == 0 else "cmp_w1_v"][l]
                    w2 = dr["cmp_w2_k" if kv == 0 else "cmp_w2_v"][l]
                    wt, nsub = wslot()
                    Rw = nsub()
                    wtv = wt[:, :].rearrange("p (c n) -> p c n", c=16)
                    DMA("pool", wtv, w1.rearrange("(c p) n -> p c n", p=128), [], [Rw])
                    w2t, Rw2 = alloc("w2t", o, BF16, 256)
                    w2v = w2t.rearrange("p (jc n) -> p jc n", jc=2)
                    for dup in range(2):
                        DMA("pool", w2v[:, :, 64 * dup:64 * dup + 64], w2.rearrange("(jc p) n -> p jc n", p=128), [], [Rw2])
                    hsv = hs.rearrange("p (jc n) -> p jc n", jc=2)
                    sgc, Rsgc = alloc("sgc", o + 512, F32, 128)
                    for jc in range(2):
                        pb, Rpb = bank()
                        for c in range(16):
                            MM(pb[:, 0:127], wtv[:, c, 128 * jc:128 * jc + 128], A2[:, 2 * c:2 * c + 16 * 126 + 1:16],
                               c == 0, c == 15, [Rw, RA2], [Rpb])
                        ACT(sgc[:, 0:127], pb[:, 0:127], AF.Exp, [Rpb, Rcst], [Rsgc], scale=-1.0, bias=ncstT[:, l, kv, jc:jc + 1])
                        TS("dve", sgc[:, 0:127], sgc[:, 0:127], 1.0, None, ALU.add, None, [Rsgc], [Rsgc])
                        RECIP(sgc[:, 0:127], sgc[:, 0:127], [Rsgc], [Rsgc])
                        STT(hsv[:, jc, 0:127], pb[:, 0:127], cstT[:, l, kv, jc:jc + 1], sgc[:, 0:127], ALU.add, ALU.mult,
                            [Rpb, Rcst, Rsgc], [Rhs])
                    if kv == 0:
                        pb, Rpb = bank()
                        for jc in range(2):
                            MM(pb[:, 0:127], w2v[:, jc, :], hsv[:, jc, 0:127], jc == 0, jc == 1, [Rw2, Rhs], [Rpb])
                        norm_rope(pb, Rpb, pm["p_kg"][:, l, 0:1], cb["c_cosC"][:], cb["c_sinC"][:], [Rcon],
                                  [(kcT[0:64, 0:127], slice(0, 64))], RkcT, 127)
                    else:
                        pb, Rpb = bank()
                        for jc in range(2):
                            MM(pb[0:127, 0:64], hsv[:, jc, 0:127], w2v[:, jc, 0:64], jc == 0, jc == 1, [Rw2, Rhs], [Rpb])
                        CP("act", Rt[0:127, 0:64], pb[0:127, 0:64], [Rpb], [RRt])
                if dbg and g == 0:
                    dump("nsa_qT", qT, [RqT])
                    dump("nsa_kTs", kTs, [RkTs])
                    dump("nsa_kcT", kcT, [RkcT])
                    dump("nsa_Rt", Rt, [RRt])
                    dump("nsa_gate", gate, [Rgate])
                    dump("nsa_vS", vS, [RvS])

                if NSA_STOP <= 3:
                    continue
                o = TMP0 + 30 * 1024
                eTs = []
                for k in range(3):
                    a_, r_ = alloc("eT%d" % k, o, BF16, 512)
                    o += 1024
                    eTs.append((a_, r_))
                err = [0]
                n2 = []
                for k in range(2):
                    d_ = {}
                    for nm, dt_, n_ in (("oacc", F32, 256), ("obf", BF16, 256), ("negT", BF16, 128), ("rs", F32, 4),
                                        ("coef", F32, 4), ("imp", F32, 32), ("sc", F32, 32), ("sc2", F32, 32),
                                        ("m8", F32, 16), ("negm", BF16, 128)):
                        a_, r_ = alloc("%s%d" % (nm, k), o, dt_, n_)
                        o += n_ * (4 if dt_ == F32 else 2)
                        d_[nm] = (a_, r_)
                    n2.append(d_)
                    MEMSET("pool", d_["negm"][0], 0.0, [d_["negm"][1]])
                hg_of_hb = [0, 1, 2, 3]

                def qk(pb, Rpb, kT, RkT, ksl, nk, i):
                    MM(pb[0:nk, :], kT[:, ksl], qTv[:, i, :, :], True, False, [RkT, RqT, RqU[i]], [Rpb], skip=True)

                def bc4(ap):
                    return ap.unsqueeze(1).to_broadcast([ap.shape[0], 4, ap.shape[1]])

                for i in range(NT):
                    nb_ = n2[i % 2]
                    oacc, Roacc = nb_["oacc"]
                    obf, Robf = nb_["obf"]
                    negT, RnegT = nb_["negT"]
                    rs, Rrs = nb_["rs"]
                    coef, Rcoef = nb_["coef"]
                    imp, Rimp = nb_["imp"]
                    sc, Rsc = nb_["sc"]
                    sc2, Rsc2 = nb_["sc2"]
                    m8, Rm8 = nb_["m8"]
                    negm, Rnegm = nb_["negm"]
                    oaccv = oacc.rearrange("p (h d) -> p h d", h=4)
                    P.tag = "nsa_cmp"
                    nv = min(127, 8 * i + 7)
                    pb, Rpb = bank()
                    qk(pb, Rpb, kcT, RkcT, slice(0, nv), nv, i)
                    MM(pb[0:nv, :], ident_b[0:nv, 0:nv], bc4(cb["c_cmpneg"][0:nv, tsl(i)]), False, True, [Rcon], [Rpb], skip=True)
                    eT, ReT = eTs[err[0] % 3]
                    err[0] += 1
                    ACT(eT[0:nv, :], pb[0:nv, :], AF.Exp, [Rpb], [ReT], scale=0.125)
                    poc, Rpoc = bank()
                    for hb in range(4):
                        MM(poc[:, 128 * hb:128 * hb + 97], eT[0:nv, 128 * hb:128 * hb + 128], Rt[0:nv, 0:97], hb == 0, hb == 3,
                           [ReT, RRt], [Rpoc], skip=True)
                    pocv = poc[:, :].rearrange("p (h c) -> p h c", h=4)
                    ACT(rs, pocv[:, :, 64], AF.Ln, [Rpoc], [Rrs], bias=1e-18)
                    ACT(rs, rs, AF.Exp, [Rrs], [Rrs], scale=-1.0)
                    TT("dve", coef, rs, gatev[:, i, 0:10:3], ALU.mult, [Rrs, Rgate], [Rcoef])
                    for hb in range(4):
                        hg_ = hg_of_hb[hb]
                        TS("dve", oaccv[:, hg_, :], pocv[:, hb, 0:64], coef[:, hb:hb + 1], None, ALU.mult, None,
                           [Rpoc, Rcoef], [Roacc])
                    if i >= 8 and NSA_STOP > 4:
                        TS("dve", imp, pocv[:, 0, 65:97], rs[:, 0:1], None, ALU.mult, None, [Rpoc, Rrs], [Rimp])
                        for hb in range(1, 4):
                            STT(imp, pocv[:, hb, 65:97], rs[:, hb:hb + 1], imp, ALU.mult, ALU.add, [Rpoc, Rrs, Rimp], [Rimp])
                        TT("dve", sc, imp, cb["c_C"][:, i - 8, :], ALU.add, [Rimp, Rcon], [Rsc])
                        P.op("dve", lambda e, m8=m8, sc=sc: e.max(out=m8[:, 0:8], in_=sc), [Rsc], [Rm8])
                        P.op("dve", lambda e, m8=m8, sc=sc, sc2=sc2: e.match_replace(out=sc2, in_to_replace=m8[:, 0:8],
                                                                                     in_values=sc, imm_value=-3e38),
                             [Rsc, Rm8], [Rsc2])
                        P.op("dve", lambda e, m8=m8, sc2=sc2: e.max(out=m8[:, 8:16], in_=sc2), [Rsc2], [Rm8])
                        TS("dve", sc2, sc, m8[:, 15:16], None, ALU.is_ge, None, [Rsc, Rm8], [Rsc2])
                        TS("dve", negm[:, 64:96], sc2, -1.0, -NEGM, ALU.add, ALU.mult, [Rsc2], [Rnegm])
                        ptb, Rptb = bank()
                        ptv = ptb[:, :].bitcast(BF16)
                        TR(ptv[:, 0:128], negm, ident_b[:], [Rnegm, Rcon], [Rptb])
                        CP("act", qTv[64:96, i, :, :], ptv[64:96, 0:128].unsqueeze(1).to_broadcast([32, 4, 128]), [Rptb],
                           [RqU[i]])
                    for br in ((1, 2) if NSA_STOP > 6 else ((1,) if NSA_STOP > 5 else ())):
                        P.tag = "nsa_slc" if br == 1 else "nsa_swa"
                        kT, RkT = (kTs, RkTs) if br == 1 else (kTw, RkTw)
                        vv, Rvv = (vSv, RvS) if br == 1 else (vWv, RvW)
                        kts = list(range(0, i + 1)) if br == 1 else list(range(max(0, i - 4), i + 1))
                        pov, Rpov = rbank()
                        for n_, kt_ in enumerate(kts):
                            pb, Rpb = bank()
                            qk(pb, Rpb, kT, RkT, tsl(kt_), 128, i)
                            if kt_ == i:
                                MM(pb[:, :], ident_b[:], bc4(cb["c_triA"][:]), False, False, [Rcon], [Rpb], skip=True)
                            if br == 2 and kt_ == i - 4:
                                MM(pb[:, :], ident_b[:], bc4(cb["c_triB"][:]), False, False, [Rcon], [Rpb], skip=True)
                            eT, ReT = eTs[err[0] % 3]
                            err[0] += 1
                            ACT(eT, pb[:, :], AF.Exp, [Rpb], [ReT], scale=0.125)
                            for hb in range(4):
                                MM(pov[:, 128 * hb:128 * hb + 65], eT[:, 128 * hb:128 * hb + 128], vv[:, kt_, :],
                                   n_ == 0 and hb == 0, n_ == len(kts) - 1 and hb == 3, [ReT, Rvv], [Rpov], skip=True)
                        povv = pov[:, :].rearrange("p (h c) -> p h c", h=4)
                        ACT(rs, povv[:, :, 64], AF.Ln, [Rpov], [Rrs])
                        ACT(rs, rs, AF.Exp, [Rrs], [Rrs], scale=-1.0)
                        TT("dve", coef, rs, gatev[:, i, br:br + 10:3], ALU.mult, [Rrs, Rgate], [Rcoef])
                        for hb in range(4):
                            hg_ = hg_of_hb[hb]
                            if br == 1:
                                STT(oaccv[:, hg_, :], povv[:, hb, 0:64], coef[:, hb:hb + 1], oaccv[:, hg_, :], ALU.mult, ALU.add,
                                    [Rpov, Rcoef, Roacc], [Roacc])
                            else:
                                STT(obf[:, 64 * hg_:64 * hg_ + 64], povv[:, hb, 0:64], coef[:, hb:hb + 1], oaccv[:, hg_, :],
                                    ALU.mult, ALU.add, [Rpov, Rcoef, Roacc], [Robf])
                    if NSA_STOP <= 6:
                        continue
                    P.tag = "nsa_fin"
                    ptb, Rptb = bank()
                    ptv = ptb[:, :].bitcast(BF16)
                    for cc in range(2):
                        TR(ptv[:, 128 * cc:128 * cc + 128], obf[:, 128 * cc:128 * cc + 128], ident_b[:], [Robf, Rcon], [Rptb])
                    b = i // 4
                    for cc in range(2):
                        kc = 2 * g + cc
                        CP("act", AR[:, kc * T + 128 * i:kc * T + 128 * i + 128], ptv[:, 128 * cc:128 * cc + 128], [Rptb],
                           [Rmix[kc][b]])

        def ffn(l):
            P.tag = "ffn"
            P.act_barrier()
            wi = dr["w_ffn_in"][l]
            wo = dr["w_ffn_out"][l]
            wiv = wi.rearrange("(kc p) (two n) -> p kc two n", p=128, two=2)
            for hf in range(2):
                actT, Rab = alloc("actT", 0, BF16, 22 * 1024)
                Ract = [[Res("act%d_%d" % (j, bb), after=[Rab]) for bb in range(2)] for j in range(22)]
                actv = actT.rearrange("p (j t) -> p j t", j=22)
                sgs = []
                for k in range(3):
                    a_, r_ = alloc("fsg%d" % k, 52 * 1024 + 1024 * k, BF16, 512)
                    sgs.append((a_, r_))
                srr = 0
                for j in range(22):
                    wt, nsub = whalf()
                    wtv = wt.rearrange("p (kc two n) -> p kc two n", kc=8, two=2)
                    Rw2_ = [nsub(), nsub()]
                    for two_ in range(2):
                        DMA("pool", wtv[:, :, two_, :], wiv[:, :, two_, 128 * j:128 * j + 128], [], [Rw2_[two_]])
                    for bb in range(2):
                        b = 2 * hf + bb
                        pg, Rpg = bank()
                        for kc in range(8):
                            MM(pg[:, :], wtv[:, kc, 0, :], xn[:, kc, bsl(b)], kc == 0, kc == 7, [Rw2_[0], Rxn[kc][b]], [Rpg])
                        pu, Rpu = bank()
                        for kc in range(8):
                            MM(pu[:, :], wtv[:, kc, 1, :], xn[:, kc, bsl(b)], kc == 0, kc == 7, [Rw2_[1], Rxn[kc][b]], [Rpu])
                        sg, Rsg = sgs[srr % 3]
                        srr += 1
                        ACT(sg, pg[:, :], AF.Silu, [Rpg], [Rsg])
                        TT("dve", actv[:, j, 512 * bb:512 * bb + 512], pu[:, :], sg, ALU.mult, [Rpu, Rsg], [Ract[j][bb]])
                for co in range(8):
                    wt, nsub = wslot()
                    Rw = nsub()
                    wtv = wt[:, 0:22 * 128].rearrange("p (kc n) -> p kc n", kc=22)
                    DMA("pool", wtv, wo[:, 128 * co:128 * co + 128].rearrange("(kc p) n -> p kc n", p=128), [], [Rw])
                    for bb in range(2):
                        b = 2 * hf + bb
                        pb, Rpb = bank()
                        for kc in range(22):
                            MM(pb[:, :], wtv[:, kc, :], actv[:, kc, 512 * bb:512 * bb + 512], kc == 0, kc == 21,
                               [Rw, Ract[kc][bb]], [Rpb])
                        TT("dve", hT[:, co, bsl(b)], pb[:, :], hT[:, co, bsl(b)], ALU.add, [Rpb, RhT[co][b]], [RhT[co][b]])
                allact = Res("actall", after=[Ract[j][bb] for j in range(22) for bb in range(2)])
                rereg(0, 45056, allact)
            P.act_barrier()

        def compute_cst():
            for l in range(nlayers):
                for kv in range(2):
                    w1 = dr["cmp_w1_k" if kv == 0 else "cmp_w1_v"][l]
                    wt, nsub = wslot()
                    Rw = nsub()
                    wtv = wt[:, :].rearrange("p (c n) -> p c n", c=16)
                    DMA("pool", wtv, w1.rearrange("(c p) n -> p c n", p=128), [], [Rw])
                    pb, Rpb = bank()
                    for jc in range(2):
                        for c in range(16):
                            MM(pb[:, jc:jc + 1], wtv[:, c, 128 * jc:128 * jc + 128], peb[:, l, kv, c:c + 1], c == 0, c == 15,
                               [Rw, Rcon], [Rpb], skip=True)
                    CP("dve", cstT[:, l, kv, :], pb[:, 0:2], [Rpb], [Rcst])
                    TS("dve", ncstT[:, l, kv, :], pb[:, 0:2], -1.0, None, ALU.mult, None, [Rpb], [Rcst])

        compute_cst()
        for s in range(nseq):
            load_x(s)
            for l in range(nlayers):
                rmsnorm(l, 0, TMP0 + 40 * 1024)
                if dbg and s == 0 and l == 0:
                    dump("xn", xn[:, :, :], [Rxn[c][b] for c in range(8) for b in range(NB)])
                if "hg" in phases:
                    base = ares("mixbase", 0, 16384)
                    Rmix = [[Res("mix%d_%d" % (k, b), after=[base]) for b in range(NB)] for k in range(4)]
                    hgrn2(l, Rmix)
                    if dbg and s == 0 and l == 0:
                        dump("mix_hg", AR[:, 0:4 * T], [Rmix[k][b] for k in range(4) for b in range(NB)])
                    outproj(l, 0, Rmix)
                    allm = Res("mixall", after=[Rmix[k][b] for k in range(4) for b in range(NB)])
                    rereg(0, 16384, allm)
                if "nsa" in phases:
                    base = ares("mixbase", 0, 16384)
                    Rmix = [[Res("mix%d_%d" % (k, b), after=[base]) for b in range(NB)] for k in range(4)]
                    nsa(l, Rmix)
                    if dbg and s == 0 and l == 0:
                        dump("mix_nsa", AR[:, 0:4 * T], [Rmix[k][b] for k in range(4) for b in range(NB)])
                    if NSA_STOP > 6:
                        outproj(l, 1, Rmix)
                    allm = Res("mixall", after=[Rmix[k][b] for k in range(4) for b in range(NB)])
                    rereg(0, 16384, allm)
                if dbg and s == 0 and l == 0:
                    dump("h_mid", hT[:, :, :], [RhT[c][b] for c in range(8) for b in range(NB)])
                if "ffn" in phases:
                    rmsnorm(l, 1, TMP0 + 40 * 1024)
                    ffn(l)
            store_out(s)
        P.final_waits = out_dmas
        P.emit(sems)
        build_program.stats = {s: len(P.streams[s]) for s in Prog.STREAMS}
        build_program.stats["waits"] = P.nwaits
        build_program.stats["makespan_us"] = P.makespan / 1e3
        build_program.stats["busy_us"] = getattr(P, "busy", None)
        build_program.tagcost = P.tagcost
        build_program.critsum = getattr(P, "critsum", {})
    return nc


_CACHE = {}


def kernel(**inputs):
    x = np.ascontiguousarray(np.asarray(inputs["x"], dtype=np.float32))
    B = x.shape[0]
    per = B // NCORES
    if "nc" not in _CACHE:
        _CACHE["nc"] = build_program(nseq=per)
        _CACHE["consts"] = host_consts()
    nc = _CACHE["nc"]
    common = {}
    for k in WEIGHT_SHAPES:
        common[k] = np.ascontiguousarray(np.asarray(inputs[k], dtype=np.float32))
    common.update(host_params(inputs))
    common.update(_CACHE["consts"])
    in_maps = []
    for c in range(NCORES):
        m = dict(common)
        m["x"] = x[c * per:(c + 1) * per]
        in_maps.append(m)
    res = run_bass_kernel_spmd(nc, in_maps, core_ids=list(range(NCORES)))
    outs = [np.asarray(r["out"], dtype=np.float32) for r in res.results]
    return np.concatenate(outs, axis=0)
```

```python
import numpy as np
from contextlib import ExitStack
import concourse.bass as bass
import concourse.mybir as mybir
from concourse.bass_utils import run_bass_kernel_spmd

F32 = mybir.dt.float32
BF16 = mybir.dt.bfloat16
AF = mybir.ActivationFunctionType
ALU = mybir.AluOpType

T = 2048
D = 1024
NT = 16
NB = 4
DFF = 2816
NEGM = -30000.0
EPS = 1e-6
NCORES = 8
NSA_STOP = 99
SEQ_PER_CORE = 4


class Res:
    __slots__ = ("name", "w", "rs", "psum")

    def __init__(self, name, after=(), psum=False):
        self.name = name
        self.w = None
        self.rs = []
        self.psum = psum
        for o in after:
            if o.w is not None:
                self.rs.append(o.w)
            self.rs.extend(o.rs)


class Ins:
    __slots__ = ("eng", "fn", "deps", "sig", "dma", "sem", "val", "preds", "succs", "cost", "lat", "idx", "bl", "tag",
                 "start", "crit", "critkind", "pos", "slot", "ordn", "waits", "K")

    def __init__(self, eng, fn, dma, cost, lat):
        self.eng = eng
        self.fn = fn
        self.deps = []
        self.preds = []
        self.succs = []
        self.sig = False
        self.dma = dma
        self.sem = None
        self.val = None
        self.cost = cost
        self.lat = lat
        self.idx = 0
        self.bl = 0.0


class Prog:
    STREAMS = ("pe", "act", "dve", "pool", "sp")
    KDMA = 8

    def __init__(self, nc):
        self.nc = nc
        self.streams = {s: [] for s in self.STREAMS}
        self.final_waits = []
        self.nwaits = 0
        self.order = []
        self.makespan = 0.0
        self.tag = ""
        self.tagcost = {}
        self.act_prev_group = []
        self.act_group = []
        self.act_first = None

    def op(self, eng, fn, reads=(), writes=(), dma=False, cost=100.0, lat=0.0):
        ins = Ins(eng, fn, dma, cost, lat)
        self.order.append(ins)
        key = (self.tag, eng)
        self.tagcost[key] = self.tagcost.get(key, 0.0) + cost
        ins.tag = self.tag
        deps = []
        for r in reads:
            if r.w is not None:
                deps.append(r.w)
            if r.psum:
                for rr in r.rs:
                    if rr.eng != eng:
                        deps.append(rr)
        for w in writes:
            if w.w is not None:
                deps.append(w.w)
            deps.extend(w.rs)
        seen = set()
        for d in deps:
            if id(d) in seen or d is ins:
                continue
            seen.add(id(d))
            ins.preds.append(d)
            if (not d.dma) and (not dma) and d.eng == "pe" and eng == "pe":
                continue
            ins.deps.append(d)
        for r in reads:
            r.rs.append(ins)
        for w in writes:
            w.w = ins
            w.rs = []
        if eng == "act":
            if self.act_first is None:
                for p_ in self.act_prev_group:
                    if p_ not in ins.preds:
                        ins.preds.append(p_)
                self.act_first = ins
            elif self.act_first not in ins.preds:
                ins.preds.append(self.act_first)
            self.act_group.append(ins)
        self.streams[eng].append(ins)
        return ins

    def act_barrier(self):
        if self.act_group:
            self.act_prev_group = self.act_group
            self.act_group = []
            self.act_first = None

    FIXED = ()

    def schedule(self, fixed=None):
        if fixed is None:
            fixed = self.FIXED
        import heapq
        order = self.order
        n = len(order)
        for i, ins in enumerate(order):
            ins.idx = i
        for s in fixed:
            prev = None
            for ins in self.streams[s]:
                if prev is not None and prev not in ins.preds:
                    ins.preds.append(prev)
                prev = ins
        for ins in order:
            for p in ins.preds:
                p.succs.append(ins)
        for ins in reversed(order):
            m = 0.0
            for sc in ins.succs:
                if sc.bl > m:
                    m = sc.bl
            ins.bl = ins.cost + ins.lat + m
        indeg = [len(ins.preds) for ins in order]
        finish = [0.0] * n
        pending = {s: [] for s in self.STREAMS}
        readyh = {s: [] for s in self.STREAMS}
        eng_free = {s: 0.0 for s in self.STREAMS}

        def push(ins):
            rt = 0.0
            for p in ins.preds:
                t = finish[p.idx] + p.lat + (100.0 if p.eng != ins.eng else (0.0 if ins.eng == "pe" else 40.0))
                if t > rt:
                    rt = t
            heapq.heappush(pending[ins.eng], (rt, -ins.bl, ins.idx))

        for ins in order:
            if indeg[ins.idx] == 0:
                push(ins)
        new_streams = {s: [] for s in self.STREAMS}
        self.gorder = []
        done = 0
        while done < n:
            best = None
            for s in self.STREAMS:
                t = eng_free[s]
                pd = pending[s]
                rd = readyh[s]
                while pd and pd[0][0] <= t:
                    rt, nb, ix = heapq.heappop(pd)
                    heapq.heappush(rd, (nb, ix, rt))
                if rd:
                    cs = t
                elif pd:
                    cs = pd[0][0]
                else:
                    continue
                if best is None or cs < best[0]:
                    best = (cs, s)
            cs, s = best
            if readyh[s]:
                nb, ix, rt = heapq.heappop(readyh[s])
            else:
                rt, nb, ix = heapq.heappop(pending[s])
            ins = order[ix]
            st = max(eng_free[s], rt)
            ins.start = st
            if eng_free[s] >= rt:
                ins.crit = new_streams[s][-1] if new_streams[s] else None
                ins.critkind = "eng"
            else:
                bp, bt = None, -1.0
                for p in ins.preds:
                    t = finish[p.idx] + p.lat + (100.0 if p.eng != ins.eng else (0.0 if ins.eng == "pe" else 40.0))
                    if t > bt:
                        bp, bt = p, t
                ins.crit = bp
                ins.critkind = "dep"
            fin = st + ins.cost
            finish[ix] = fin
            eng_free[s] = fin
            new_streams[s].append(ins)
            self.gorder.append(ins)
            done += 1
            for sc in ins.succs:
                indeg[sc.idx] -= 1
                if indeg[sc.idx] == 0:
                    push(sc)
        self.streams = new_streams
        self.makespan = max(finish) if n else 0.0
        self.busy = {s: sum(i.cost for i in new_streams[s]) / 1e3 for s in self.STREAMS}
        last = max(order, key=lambda i: finish[i.idx]) if n else None
        summ = {}
        cur = last
        while cur is not None:
            prev = cur.crit
            t_prev = (finish[prev.idx] if prev is not None else 0.0)
            seg = finish[cur.idx] - t_prev
            key = (cur.tag, cur.eng, cur.critkind + ("<-" + prev.eng + ("*dma" if prev.dma else "") if (prev is not None and cur.critkind == "dep") else ""))
            summ[key] = summ.get(key, 0.0) + seg
            cur = prev
        self.critsum = summ

    def emit(self, sems, do_schedule=True):
        nc = self.nc
        if do_schedule:
            self.schedule()
        for s in self.STREAMS:
            hist = []
            for ins in self.streams[s]:
                if ins.dma:
                    if len(hist) >= self.KDMA and hist[-self.KDMA] not in ins.deps:
                        ins.deps.append(hist[-self.KDMA])
                    hist.append(ins)
        gorder = getattr(self, "gorder", None) or self.order
        for s in self.STREAMS:
            nd = 0
            for p_, ins in enumerate(self.streams[s]):
                ins.pos = p_
                if ins.dma:
                    ins.slot = nd % self.KDMA
                    ins.ordn = nd // self.KDMA + 1
                    nd += 1
        have_k = {s: {} for s in self.STREAMS}
        for ins in gorder:
            h = have_k[ins.eng]
            waits = []
            for d in ins.deps:
                key = ("d", d.eng, d.slot) if d.dma else ("c", d.eng)
                need = d.ordn if d.dma else d.pos + 1
                if h.get(key, 0) >= need:
                    continue
                waits.append(d)
                for k2, v2 in d.K.items():
                    if h.get(k2, 0) < v2:
                        h[k2] = v2
            ins.waits = waits
            K = dict(h)
            if ins.dma:
                K[("d", ins.eng, ins.slot)] = ins.ordn
            else:
                K[("c", ins.eng)] = ins.pos + 1
            ins.K = K
        for ins in gorder:
            for d in ins.waits:
                d.sig = True
            ins.K = None
        for s in self.STREAMS:
            cnt = 0
            nd = 0
            for ins in self.streams[s]:
                if ins.dma:
                    ins.sem = sems["d_%s_%d" % (s, nd % self.KDMA)]
                    ins.val = 16 * (nd // self.KDMA + 1)
                    ins.sig = True
                    nd += 1
                elif ins.sig:
                    cnt += 1
                    ins.sem = sems["c_" + s]
                    ins.val = cnt

        def run(stream, e):
            have = {}
            for ins in self.streams[stream]:
                for d in ins.waits:
                    k = id(d.sem)
                    if have.get(k, 0) >= d.val:
                        continue
                    have[k] = d.val
                    e.wait_ge(d.sem, d.val)
                    self.nwaits += 1
                bi = ins.fn(e)
                if ins.sig:
                    bi.then_inc(ins.sem, 16 if ins.dma else 1)
            if stream == "sp":
                for d in self.final_waits:
                    k = id(d.sem)
                    if have.get(k, 0) >= d.val:
                        continue
                    have[k] = d.val
                    e.wait_ge(d.sem, d.val)

        with nc.Block() as block:
            @block.tensor
            def _(e):
                run("pe", e)

            @block.scalar
            def _(e):
                run("act", e)

            @block.vector
            def _(e):
                run("dve", e)

            @block.gpsimd
            def _(e):
                run("pool", e)

            @block.sync
            def _(e):
                run("sp", e)


def host_consts():
    c = {}
    f32 = np.float32
    c["c_ident"] = np.eye(128, dtype=f32)
    c["c_ones"] = np.ones((128, 128), f32)
    bd = np.zeros((128, 128), f32)
    bd[:64, :64] = 1
    bd[64:, 64:] = 1
    c["c_bd"] = bd
    prot = np.zeros((128, 128), f32)
    for m in range(128):
        mm = m % 64
        base = m - mm
        if mm < 32:
            prot[base + mm + 32, m] = -1.0
        else:
            prot[base + mm - 32, m] = 1.0
    c["c_prot"] = prot
    s = np.arange(128)[:, None]
    t = np.arange(128)[None, :]
    c["c_triA"] = np.where(s <= t, 0.0, NEGM).astype(f32)
    c["c_triB"] = np.where(s > t, 0.0, NEGM).astype(f32)
    same = (s // 64) == (t // 64)
    c["c_hgmask"] = (same & (s <= t)).astype(f32)
    L = np.zeros((128, 132), f32)
    sl = s % 64
    tl = t % 64
    L[:, :128] = np.where(same, (sl <= tl).astype(f32) - (sl <= 31).astype(f32), 0.0)
    sv = np.arange(128)
    L[:, 128] = ((sv < 64) & (sv % 64 <= 31)).astype(f32)
    L[:, 129] = ((sv >= 64) & (sv % 64 <= 31)).astype(f32)
    L[:, 130] = (sv < 64).astype(f32)
    L[:, 131] = (sv >= 64).astype(f32)
    c["c_L"] = L
    c["c_U"] = (same & (s > t)).astype(f32)
    E = np.zeros((128, T), f32)
    for j in range(32):
        E[j, 64 * j:64 * j + 64] = 1.0
    c["c_E"] = E
    n = np.arange(127)[:, None]
    tt = np.arange(T)[None, :]
    cm = np.full((128, T), NEGM, f32)
    cm[:127] = np.where(16 * n + 31 <= tt, 0.0, NEGM)
    c["c_cmpneg"] = cm
    C = np.zeros((128, 8, 32), f32)
    for ii in range(8):
        for tl_ in range(128):
            tq = 128 * (8 + ii) + tl_
            cur = tq // 64
            for j in range(32):
                if j == 0:
                    v = 1e9
                elif j == cur:
                    v = 2e9
                elif j == cur - 1:
                    v = 3e9
                elif j <= cur:
                    v = 0.0
                else:
                    v = -1e30
                C[tl_, ii, j] = v
    c["c_C"] = C
    half = 32
    freqs = (np.float32(10000.0) ** (-np.arange(half, dtype=f32) / np.float32(half))).astype(f32)
    pos = np.arange(T, dtype=f32)
    ang = (pos[:, None] * freqs[None, :]).astype(f32)
    cosv = np.cos(ang).astype(f32)
    sinv = np.sin(ang).astype(f32)
    pidx = (np.arange(128) % 64) % 32
    c["c_cos"] = np.ascontiguousarray(cosv[:, pidx].T)
    c["c_sin"] = np.ascontiguousarray(sinv[:, pidx].T)
    ce = np.arange(127) * 16 + 31
    c["c_cosC"] = np.ascontiguousarray(c["c_cos"][:, ce])
    c["c_sinC"] = np.ascontiguousarray(c["c_sin"][:, ce])
    c_start = np.arange(127) * 16
    c_end = c_start + 31
    s_start = np.arange(32) * 64
    s_end = s_start + 63
    M = ((c_start[:, None] <= s_end[None, :]) & (c_end[:, None] >= s_start[None, :])).astype(f32)
    rt = np.zeros((128, 33), f32)
    rt[:127, 0] = 1.0
    rt[:127, 1:] = M
    c["c_rtail"] = rt
    return c


CONST_SHAPES = {
    "c_ident": [128, 128], "c_ones": [128, 128], "c_bd": [128, 128], "c_prot": [128, 128],
    "c_triA": [128, 128], "c_triB": [128, 128], "c_hgmask": [128, 128], "c_L": [128, 132],
    "c_U": [128, 128], "c_E": [128, T], "c_cmpneg": [128, T], "c_C": [128, 8, 32],
    "c_cos": [128, T], "c_sin": [128, T], "c_cosC": [128, 127], "c_sinC": [128, 127],
    "c_rtail": [128, 33],
}


def host_params(inp):
    p = {}
    f = np.float32
    nm = np.asarray(inp["norm_mix"], f).reshape(2, 8, 128)
    nf = np.asarray(inp["norm_ffn"], f).reshape(2, 8, 128)
    nrm = np.zeros((128, 2, 2, 8), f)
    nrm[:, :, 0, :] = nm.transpose(2, 0, 1)
    nrm[:, :, 1, :] = nf.transpose(2, 0, 1)
    p["p_nrm"] = nrm
    p["p_gn"] = np.ascontiguousarray(np.asarray(inp["hg_gnorm"], f).T)
    p["p_lbl"] = np.ascontiguousarray(np.asarray(inp["hg_lb_logits"], f).reshape(2, 4, 128).transpose(2, 0, 1))
    qg = np.asarray(inp["q_gain"], f)
    p["p_qg"] = np.ascontiguousarray(np.concatenate([qg, qg], axis=1).T)
    kg = np.asarray(inp["k_gain"], f)
    p["p_kg"] = np.ascontiguousarray(np.concatenate([kg, kg], axis=2).transpose(2, 0, 1))
    pk = np.asarray(inp["cmp_pe_k"], f).reshape(2, 16, 128)
    pv = np.asarray(inp["cmp_pe_v"], f).reshape(2, 16, 128)
    pe = np.zeros((128, 2, 2, 16), f)
    pe[:, :, 0, :] = pk.transpose(2, 0, 1)
    pe[:, :, 1, :] = pv.transpose(2, 0, 1)
    p["p_pe"] = pe
    return p


PARAM_SHAPES = {"p_nrm": [128, 2, 2, 8], "p_gn": [128, 2], "p_lbl": [128, 2, 4], "p_qg": [128, 2],
                "p_kg": [128, 2, 3], "p_pe": [128, 2, 2, 16]}

WEIGHT_SHAPES = {"w_in": [2, 1024, 3352], "w_out": [2, 1024, 1024], "cmp_w1_k": [2, 2048, 256],
                 "cmp_w2_k": [2, 256, 64], "cmp_w1_v": [2, 2048, 256], "cmp_w2_v": [2, 256, 64],
                 "w_ffn_in": [2, 1024, 2 * DFF], "w_ffn_out": [2, DFF, 1024]}


def build_program(nseq=SEQ_PER_CORE, nlayers=2, dbg=None, phases=("hg", "nsa", "ffn")):
    nc = bass.Bass("TRN2", target_bir_lowering=False)
    dr = {}
    dr["x"] = nc.dram_tensor("x", [nseq, T, D], F32, kind="ExternalInput").ap()
    for k, sh in WEIGHT_SHAPES.items():
        dr[k] = nc.dram_tensor(k, sh, F32, kind="ExternalInput").ap()
    for k, sh in PARAM_SHAPES.items():
        dr[k] = nc.dram_tensor(k, sh, F32, kind="ExternalInput").ap()
    for k, sh in CONST_SHAPES.items():
        dr[k] = nc.dram_tensor(k, sh, F32, kind="ExternalInput").ap()
    out_d = nc.dram_tensor("out", [nseq, T, D], F32, kind="ExternalOutput").ap()
    dbg_d = {}
    if dbg:
        for name, (shape, dt) in dbg.items():
            dbg_d[name] = nc.dram_tensor("dbg_" + name, shape, dt, kind="ExternalOutput").ap()

    es = ExitStack()
    with es:
        def sb(name, shape, dt):
            return es.enter_context(nc.sbuf_tensor("s_" + name, shape, dt))

        P = Prog(nc)
        sems = {}
        for s in Prog.STREAMS:
            sems["c_" + s] = es.enter_context(nc.semaphore("c_" + s))
            for k in range(Prog.KDMA):
                sems["d_%s_%d" % (s, k)] = es.enter_context(nc.semaphore("d_%s_%d" % (s, k)))

        hT = sb("hT", [128, 8, T], F32)
        xn = sb("xn", [128, 8, T], BF16)
        ARB = 78 * 1024
        AR = sb("arena", [128, ARB // 2], BF16)
        TMP0 = 16 * 1024
        WB = [sb("wb0", [128, 4096], BF16), sb("wb1", [128, 4096], BF16)]
        RWB = [Res("wb0"), Res("wb1")]
        PS = [es.enter_context(nc.psum_tensor("ps%d" % k, [128, 512], F32)) for k in range(8)]
        RPS = [Res("ps%d" % k, psum=True) for k in range(8)]

        cb = {}
        for k in ("c_ident", "c_ones", "c_bd", "c_prot", "c_triA", "c_triB", "c_hgmask", "c_rtail"):
            cb[k] = sb(k, CONST_SHAPES[k], BF16)
        cb["c_E"] = sb("c_E", [128, T], BF16)
        cb["c_cmpneg"] = sb("c_cmpneg", [128, T], BF16)
        for k in ("c_L", "c_U", "c_C", "c_cosC", "c_sinC"):
            cb[k] = sb(k, CONST_SHAPES[k], F32)
        cb["c_identf"] = sb("c_identf", [128, 128], F32)
        pm = {}
        for k, sh in PARAM_SHAPES.items():
            pm[k] = sb(k, sh, F32)
        peb = sb("peb", [128, 2, 2, 16], BF16)
        lbF = sb("lbF", [128, 2, 4], F32)
        omlbF = sb("omlbF", [128, 2, 4], F32)
        lbt = sb("lbt", [128, 4], F32)
        cstT = sb("cstT", [128, 2, 2, 2], F32)
        ncstT = sb("ncstT", [128, 2, 2, 2], F32)
        Rcon = Res("consts")
        Rcst = Res("cst")

        arena_regs = []

        def carve(off, dt, n):
            if dt == F32:
                return AR[:, off // 2: off // 2 + 2 * n].bitcast(F32)
            return AR[:, off // 2: off // 2 + n]

        def ares(name, off, nbytes):
            st, en = off, off + nbytes
            old = [r for (s_, e_, r) in arena_regs if s_ < en and st < e_]
            res = Res(name, after=old)
            keep = []
            for (s_, e_, r) in arena_regs:
                if s_ < en and st < e_:
                    if s_ < st:
                        keep.append((s_, st, r))
                    if en < e_:
                        keep.append((en, e_, r))
                else:
                    keep.append((s_, e_, r))
            arena_regs[:] = keep
            arena_regs.append((st, en, res))
            return res

        def rereg(st, en, res):
            keep = []
            for (s_, e_, r) in arena_regs:
                if s_ < en and st < e_:
                    if s_ < st:
                        keep.append((s_, st, r))
                    if en < e_:
                        keep.append((en, e_, r))
                else:
                    keep.append((s_, e_, r))
            arena_regs[:] = keep
            arena_regs.append((st, en, res))

        def alloc(name, off, dt, n):
            nb = n * (4 if dt == F32 else 2)
            assert off + nb <= ARB, (name, off, nb)
            return carve(off, dt, n), ares(name, off, nb)

        def ecost(eng, n):
            if eng == "act":
                return 190.0 + 0.83 * n
            if eng == "pool":
                return 160.0 + 1.7 * n
            return 70.0 + 1.0 * n

        def MM(out, lhsT, rhs, st, sp, R, W, skip=False):
            n = rhs.free_size()
            c = (100.0 + 0.3 * max(64, n)) * (2.5 if rhs.dtype == F32 else 1.0)
            P.op("pe", lambda e: e.matmul(out, lhsT=lhsT, rhs=rhs, start=st, stop=sp, skip_group_check=skip), R, W, cost=c)

        def TR(out, in_, ident, R, W):
            c = 280.0 if in_.dtype == F32 else 90.0
            P.op("pe", lambda e: e.transpose(out, in_, ident), R, W, cost=c)

        def ACT(out, in_, func, R, W, scale=1.0, bias=0.0):
            P.op("act", lambda e: e.activation(out=out, in_=in_, func=func, scale=scale, bias=bias), R, W,
                 cost=ecost("act", in_.free_size()))

        def TT(eng, out, in0, in1, op, R, W):
            P.op(eng, lambda e: e.tensor_tensor(out=out, in0=in0, in1=in1, op=op), R, W, cost=ecost(eng, in0.free_size()))

        def TS(eng, out, in0, s1, s2, op0, op1, R, W):
            c = ecost(eng, in0.free_size())
            if s2 is None:
                P.op(eng, lambda e: e.tensor_scalar(out=out, in0=in0, scalar1=s1, scalar2=None, op0=op0), R, W, cost=c)
            else:
                P.op(eng, lambda e: e.tensor_scalar(out=out, in0=in0, scalar1=s1, scalar2=s2, op0=op0, op1=op1), R, W, cost=c)

        def STT(out, in0, scalar, in1, op0, op1, R, W):
            P.op("dve", lambda e: e.scalar_tensor_tensor(out=out, in0=in0, scalar=scalar, in1=in1, op0=op0, op1=op1), R, W,
                 cost=ecost("dve", in0.free_size()))

        def CP(eng, out, in_, R, W):
            c = ecost(eng, in_.free_size())
            if eng == "act":
                P.op("act", lambda e: e.copy(out=out, in_=in_), R, W, cost=c)
            else:
                P.op(eng, lambda e: e.tensor_copy(out=out, in_=in_), R, W, cost=c)

        def RECIP(out, in_, R, W):
            P.op("dve", lambda e: e.reciprocal(out=out, in_=in_), R, W, cost=ecost("dve", in_.free_size()))

        def MEMSET(eng, ap, val, W):
            P.op(eng, lambda e: e.memset(ap, val), (), W, cost=ecost(eng, ap.free_size()))

        def DMA(eng, out, in_, R, W, slow=False):
            lat = 2000.0 + max(out.nbytes(), in_.nbytes()) / 180.0
            if slow:
                return P.op(eng, lambda e: e.dma_start(out=out, in_=in_, allow_slow_non_contiguous=True), R, W, dma=True,
                            cost=60.0, lat=lat)
            return P.op(eng, lambda e: e.dma_start(out=out, in_=in_), R, W, dma=True, cost=60.0, lat=lat)

        Rdram = Res("dram_out")
        out_dmas = []

        def dump(name, ap, R):
            if dbg and name in dbg_d:
                out_dmas.append(DMA("sp", dbg_d[name], ap, R, [Rdram]))
                del dbg_d[name]

        psrr = [0]

        def bank():
            k = psrr[0] % 6
            psrr[0] += 1
            return PS[k], RPS[k]

        resrr = [0]

        def rbank():
            k = 6 + resrr[0] % 2
            resrr[0] += 1
            return PS[k], RPS[k]

        for k in ("c_ident", "c_ones", "c_bd", "c_prot", "c_triA", "c_triB", "c_hgmask", "c_rtail", "c_cmpneg"):
            DMA("pool", cb[k][:], dr[k], [], [Rcon])
        DMA("pool", cb["c_E"][:], dr["c_E"], [], [Rcon])
        for k in ("c_L", "c_U", "c_C", "c_cosC", "c_sinC"):
            DMA("sp", cb[k][:], dr[k], [], [Rcon])
        DMA("sp", cb["c_identf"][:], dr["c_ident"], [], [Rcon])
        for k in PARAM_SHAPES:
            DMA("sp", pm[k][:], dr[k], [], [Rcon])
        DMA("pool", peb[:], dr["p_pe"], [], [Rcon])
        ident_b = cb["c_ident"]
        ident_f = cb["c_identf"]
        ones_b = cb["c_ones"]
        TT("dve", lbt[:], pm["p_lbl"][:, 1, :], pm["p_lbl"][:, 0, :], ALU.subtract, [Rcon], [Rcon])
        ACT(lbt[:], lbt[:], AF.Exp, [Rcon], [Rcon], scale=-1.0)
        TS("dve", lbt[:], lbt[:], 1.0, None, ALU.add, None, [Rcon], [Rcon])
        RECIP(lbF[:, 1, :], lbt[:], [Rcon], [Rcon])
        MEMSET("dve", lbF[:, 0, :], 0.0, [Rcon])
        TS("dve", omlbF[:], lbF[:], -1.0, 1.0, ALU.mult, ALU.add, [Rcon], [Rcon])

        RhT = [[Res("hT%d_%d" % (c, b)) for b in range(NB)] for c in range(8)]
        Rxn = [[Res("xn%d_%d" % (c, b)) for b in range(NB)] for c in range(8)]

        def bsl(b):
            return slice(512 * b, 512 * b + 512)

        def tsl(i):
            return slice(128 * i, 128 * i + 128)

        wrr = [0]

        wsubs = [[], [], [], []]

        def _claim(minis):
            olds = []
            for m_ in minis:
                olds.extend(wsubs[m_])
            base = Res("wbase", after=olds)
            for m_ in minis:
                wsubs[m_] = []

            def newsub():
                r_ = Res("wsub", after=[base])
                for m_ in minis:
                    wsubs[m_].append(r_)
                return r_
            return newsub

        def wslot():
            k = wrr[0] % 2
            wrr[0] += 1
            return WB[k], _claim([2 * k, 2 * k + 1])

        hrr = [0]

        def whalf():
            m_ = hrr[0] % 4
            hrr[0] += 1
            return WB[m_ // 2][:, 2048 * (m_ % 2):2048 * (m_ % 2) + 2048], _claim([m_])

        def load_x(s):
            P.tag = "load"
            for i in range(NT):
                xin, Rxin = alloc("xin%d" % (i % 4), TMP0 + (i % 4) * 4096, F32, 1024)
                DMA("sp", xin, dr["x"][s, tsl(i), :], [], [Rxin])
                for half in range(2):
                    pb, Rpb = bank()
                    for cc in range(4):
                        c = half * 4 + cc
                        TR(pb[:, 128 * cc:128 * cc + 128], xin[:, 128 * c:128 * c + 128], ident_f[:], [Rxin, Rcon], [Rpb])
                    dst = hT[:, half * 4:half * 4 + 4, tsl(i)]
                    src = pb[:, :].rearrange("p (c t) -> p c t", c=4)
                    b = i // 4
                    W = [RhT[half * 4 + cc][b] for cc in range(4)]
                    if half == 0:
                        CP("act", dst, src, [Rpb], W)
                    else:
                        CP("dve", dst, src, [Rpb], W)

        def store_out(s):
            P.tag = "store"
            for i in range(NT):
                xo, Rxo = alloc("xo%d" % (i % 4), TMP0 + (i % 4) * 4096, F32, 1024)
                b = i // 4
                for half in range(2):
                    pb, Rpb = bank()
                    for cc in range(4):
                        c = half * 4 + cc
                        TR(pb[:, 128 * cc:128 * cc + 128], hT[:, c, tsl(i)], ident_f[:], [RhT[c][b], Rcon], [Rpb])
                    if half == 0:
                        CP("act", xo[:, 0:512], pb[:, :], [Rpb], [Rxo])
                    else:
                        CP("dve", xo[:, 512:1024], pb[:, :], [Rpb], [Rxo])
                out_dmas.append(DMA("sp", out_d[s, tsl(i), :], xo, [Rxo], [Rdram]))

        def rmsnorm(l, which, toff):
            P.tag = "norm"
            for b in range(NB):
                sq, Rsq = alloc("nsq", toff, BF16, 8 * 512)
                rs, Rrs = alloc("nrs", toff + 8192, F32, 512)
                pb, Rpb = bank()
                for c in range(8):
                    ACT(sq[:, 512 * c:512 * c + 512], hT[:, c, bsl(b)], AF.Square, [RhT[c][b]], [Rsq])
                for c in range(8):
                    MM(pb[:, :], ones_b[:], sq[:, 512 * c:512 * c + 512], c == 0, c == 7, [Rsq, Rcon], [Rpb])
                ACT(rs, pb[:, :], AF.Ln, [Rpb], [Rrs], scale=1.0 / D, bias=EPS)
                ACT(rs, rs, AF.Exp, [Rrs], [Rrs], scale=-0.5)
                for c in range(8):
                    STT(xn[:, c, bsl(b)], hT[:, c, bsl(b)], pm["p_nrm"][:, l, which, c:c + 1], rs,
                        ALU.mult, ALU.mult, [RhT[c][b], Rrs, Rcon], [Rxn[c][b]])

        def hgrn2(l, Rmix):
            P.tag = "hg_proj"
            win = dr["w_in"][l]
            wv = win[:, 0:2048].rearrange("(kc p) (j hh n) -> p kc j hh n", p=128, j=4, hh=4)
            for h in range(4):
                P.tag = "hg_proj"
                wt, nsub = wslot()
                wtv = wt[:, :].rearrange("p (kc j n) -> p kc j n", kc=8, j=4)
                Rwj = [nsub() for _ in range(4)]
                for j_ in range(4):
                    DMA("pool", wtv[:, :, j_, :], wv[:, :, j_, h, :], [], [Rwj[j_]])
                o = TMP0
                def blocked(name, off, dt):
                    esz = 4 if dt == F32 else 2
                    aps, rss = [], []
                    for b_ in range(NB):
                        a_, r_ = alloc("%s_%d" % (name, b_), off + 512 * esz * b_, dt, 512)
                        aps.append(a_)
                        rss.append(r_)
                    return carve(off, dt, T), rss
                qs, RqsB = blocked("qs", o, BF16)
                gs, RgsB = blocked("gs", o + 4096, BF16)
                vt, RvtB = blocked("vt", o + 8192, BF16)
                lfT, RlfTB = blocked("lfT", o + 12288, F32)
                kTf, RkTfB = blocked("kTf", o + 20480, F32)
                o2 = o + 28672
                sgts = []
                for k_ in range(3):
                    a_, r_ = alloc("sgt%d" % k_, o2, F32, 512)
                    o2 += 2048
                    sgts.append((a_, r_))
                for b in range(NB):
                    sgt, Rsgt = sgts[0]
                    pb, Rpb = bank()
                    for kc in range(8):
                        MM(pb[:, :], wtv[:, kc, 0, :], xn[:, kc, bsl(b)], kc == 0, kc == 7, [Rwj[0], Rxn[kc][b]], [Rpb])
                    ACT(sgt, pb[:, :], AF.Exp, [Rpb], [Rsgt], scale=-1.0)
                    ACT(sgt, sgt, AF.Ln, [Rsgt], [Rsgt], bias=1.0)
                    ACT(sgt, sgt, AF.Exp, [Rsgt], [Rsgt], scale=-1.0)
                    TT("dve", qs[:, bsl(b)], pb[:, :], sgt, ALU.mult, [Rpb, Rsgt], [RqsB[b]])
                    sgt, Rsgt = sgts[1]
                    pb, Rpb = bank()
                    for kc in range(8):
                        MM(pb[:, :], wtv[:, kc, 3, :], xn[:, kc, bsl(b)], kc == 0, kc == 7, [Rwj[3], Rxn[kc][b]], [Rpb])
                    ACT(sgt, pb[:, :], AF.Exp, [Rpb], [Rsgt], scale=-1.0)
                    ACT(sgt, sgt, AF.Ln, [Rsgt], [Rsgt], bias=1.0)
                    ACT(sgt, sgt, AF.Exp, [Rsgt], [Rsgt], scale=-1.0)
                    TT("dve", gs[:, bsl(b)], pb[:, :], sgt, ALU.mult, [Rpb, Rsgt], [RgsB[b]])
                    sgt, Rsgt = sgts[2]
                    pb, Rpb = bank()
                    for kc in range(8):
                        MM(pb[:, :], wtv[:, kc, 1, :], xn[:, kc, bsl(b)], kc == 0, kc == 7, [Rwj[1], Rxn[kc][b]], [Rpb])
                    ACT(sgt, pb[:, :], AF.Exp, [Rpb], [Rsgt])
                    ACT(sgt, sgt, AF.Ln, [Rsgt], [Rsgt], bias=1.0)
                    ACT(sgt, sgt, AF.Exp, [Rsgt], [Rsgt], scale=-1.0)
                    TS("dve", kTf[:, bsl(b)], sgt, omlbF[:, l, h:h + 1], None, ALU.mult, None, [Rsgt, Rcon], [RkTfB[b]])
                    TS("dve", lfT[:, bsl(b)], kTf[:, bsl(b)], -1.0, 1.0, ALU.mult, ALU.add, [RkTfB[b]], [RlfTB[b]])
                    pb, Rpb = bank()
                    for ti in range(4):
                        i = 4 * b + ti
                        for kc in range(8):
                            MM(pb[:, 128 * ti:128 * ti + 128], xn[:, kc, tsl(i)], wtv[:, kc, 2, :], kc == 0, kc == 7,
                               [Rwj[2], Rxn[kc][b]], [Rpb])
                    CP("act", vt[:, bsl(b)], pb[:, :], [Rpb], [RvtB[b]])
                P.tag = "hg_rec"
                for b in range(NB):
                    ACT(lfT[:, bsl(b)], lfT[:, bsl(b)], AF.Ln, [RlfTB[b]], [RlfTB[b]])
                S, RS = alloc("S", o2, F32, 128)
                o2 += 512
                MEMSET("dve", S, 0.0, [RS])
                Sp = []
                for k in range(2):
                    a_, r_ = alloc("Sp%d" % k, o2, BF16, 128)
                    o2 += 256
                    Sp.append((a_, r_))
                tl_bufs = []
                for k in range(2):
                    d_ = {}
                    for nm, dt_, n_ in (("lft", F32, 128), ("eq", F32, 128), ("ek", F32, 128), ("ebr", F32, 128),
                                        ("qt", BF16, 128), ("kt", BF16, 128), ("kh", BF16, 128), ("AT", BF16, 128),
                                        ("eml", F32, 4)):
                        a_, r_ = alloc("%s%d" % (nm, k), o2, dt_, n_)
                        o2 += n_ * (4 if dt_ == F32 else 2)
                        d_[nm] = (a_, r_)
                    tl_bufs.append(d_)
                osq, Rosq = alloc("osq", o2, BF16, 512)
                o2 += 1024
                orr, Rorr = alloc("orr", o2, F32, 512)
                o2 += 2048
                otm, Rotm = alloc("otm", o2, F32, 512)
                o2 += 2048
                pbo = None
                for i in range(NT):
                    tb = tl_bufs[i % 2]
                    lft, Rlft = tb["lft"]
                    eq, Req = tb["eq"]
                    ek, Rek = tb["ek"]
                    ebr, Rebr = tb["ebr"]
                    qt, Rqt = tb["qt"]
                    kt, Rkt = tb["kt"]
                    kh, Rkh = tb["kh"]
                    AT_, RAT = tb["AT"]
                    eml, Reml = tb["eml"]
                    b = i // 4
                    Rqs, Rgs, Rvt, RlfT, RkTf = RqsB[b], RgsB[b], RvtB[b], RlfTB[b], RkTfB[b]
                    ptr, Rptr = bank()
                    TR(ptr[:, 0:128], lfT[:, tsl(i)], ident_f[:], [RlfT, Rcon], [Rptr])
                    TR(ptr[:, 128:256], kTf[:, tsl(i)], ident_f[:], [RkTf, Rcon], [Rptr])
                    CP("act", lft, ptr[:, 0:128], [Rptr], [Rlft])
                    pcs, Rpcs = bank()
                    MM(pcs[:, 0:132], lft, cb["c_L"][:], True, True, [Rlft, Rcon], [Rpcs])
                    MM(pcs[:, 256:384], cb["c_U"][:], lft, True, True, [Rlft, Rcon], [Rpcs], skip=True)
                    ACT(eq, pcs[:, 0:128], AF.Exp, [Rpcs], [Req])
                    ACT(ek, pcs[:, 0:128], AF.Exp, [Rpcs], [Rek], scale=-1.0)
                    ACT(eml, pcs[:, 128:132], AF.Exp, [Rpcs], [Reml])
                    ACT(ebr, pcs[:, 256:384], AF.Exp, [Rpcs], [Rebr])
                    TT("dve", qt, qs[:, tsl(i)], eq, ALU.mult, [Rqs, Req], [Rqt])
                    TT("dve", kt, kTf[:, tsl(i)], ek, ALU.mult, [RkTf, Rek], [Rkt])
                    TT("dve", kh, ptr[:, 128:256], ebr, ALU.mult, [Rptr, Rebr], [Rkh])
                    pa, Rpa = bank()
                    MM(pa[:, 0:128], kt, qt, True, True, [Rkt, Rqt], [Rpa])
                    TT("dve", AT_, pa[:, 0:128], cb["c_hgmask"][:], ALU.mult, [Rpa, Rcon], [RAT])
                    if i % 4 == 0:
                        pbo, Rpbo = rbank()
                    oc = pbo[:, 128 * (i % 4):128 * (i % 4) + 128]
                    MM(oc, vt[:, tsl(i)], AT_, True, False, [Rvt, RAT], [Rpbo])
                    for ch in range(2):
                        spa, Rspa = Sp[ch]
                        TS("dve", spa, S, eml[:, ch:ch + 1], None, ALU.mult, None, [RS, Reml], [Rspa])
                        MM(oc[:, 64 * ch:64 * ch + 64], spa, qt[:, 64 * ch:64 * ch + 64], False, ch == 1,
                           [Rspa, Rqt], [Rpbo])
                        pkv, Rpkv = bank()
                        MM(pkv[:, 0:128], kh[64 * ch:64 * ch + 64, :], vt[64 * ch:64 * ch + 64, tsl(i)], True, True,
                           [Rkh, Rvt], [Rpkv])
                        STT(S, S, eml[:, 2 + ch:3 + ch], pkv[:, 0:128], ALU.mult, ALU.add, [RS, Reml, Rpkv], [RS])
                    if i % 4 == 3:
                        ACT(osq, pbo[:, :], AF.Square, [Rpbo], [Rosq])
                        pss, Rpss = bank()
                        MM(pss[:, :], ones_b[:], osq, True, True, [Rosq, Rcon], [Rpss])
                        ACT(orr, pss[:, :], AF.Ln, [Rpss], [Rorr], scale=1.0 / 128, bias=EPS)
                        ACT(orr, orr, AF.Exp, [Rorr], [Rorr], scale=-0.5)
                        TT("dve", otm, pbo[:, :], orr, ALU.mult, [Rpbo, Rorr], [Rotm])
                        STT(AR[:, h * T + 512 * b:h * T + 512 * b + 512], otm, pm["p_gn"][:, l:l + 1], gs[:, bsl(b)],
                            ALU.mult, ALU.mult, [Rotm, Rcon, Rgs], [Rmix[h][b]])
                if dbg and h == 0:
                    dump("hg_qs", qs, RqsB)
                    dump("hg_lfT", lfT, RlfTB)

        def outproj(l, part, Rmix):
            P.tag = "outproj"
            wo = dr["w_out"][l]
            for cp in range(4):
                wt, nsub = wslot()
                Rw = nsub()
                wtv = wt[:, 0:1024].rearrange("p (kc n) -> p kc n", kc=4)
                src = wo[512 * part:512 * part + 512, 256 * cp:256 * cp + 256].rearrange("(kc p) n -> p kc n", p=128)
                DMA("pool", wtv, src, [], [Rw])
                for cc in range(2):
                    co = 2 * cp + cc
                    for b in range(NB):
                        pb, Rpb = bank()
                        for kc in range(4):
                            MM(pb[:, :], wtv[:, kc, 128 * cc:128 * cc + 128], AR[:, kc * T + 512 * b:kc * T + 512 * b + 512],
                               kc == 0, kc == 3, [Rw, Rmix[kc][b]], [Rpb])
                        TT("dve", hT[:, co, bsl(b)], pb[:, :], hT[:, co, bsl(b)], ALU.add, [Rpb, RhT[co][b]], [RhT[co][b]])

        def nsa(l, Rmix):
            P.tag = "nsa_n1"
            win = dr["w_in"][l]

            def wcols(c0, n):
                return win[:, c0:c0 + n].rearrange("(kc p) n -> p kc n", p=128)

            for g in range(2):
                P.tag = "nsa_n1"
                o = TMP0
                qT, RqT = alloc("qT", o, BF16, 4 * T)
                kTs, RkTs = alloc("kTs", o + 16384, BF16, T)
                kTw, RkTw = alloc("kTw", o + 20480, BF16, T)
                vS, RvS = alloc("vS", o + 24576, BF16, 16 * 65)
                vW, RvW = alloc("vW", o + 24576 + 2080, BF16, 16 * 65)
                gate, Rgate = alloc("gate", o + 24576 + 4160, F32, 16 * 12)
                kcT, RkcT = alloc("kcT", o + 24576 + 4160 + 768, BF16, 128)
                Rt, RRt = alloc("Rt", o + 24576 + 4160 + 768 + 256, BF16, 98)
                o = TMP0 + 30 * 1024
                qTv = qT.rearrange("p (i h t) -> p i h t", i=NT, h=4)
                vSv = vS.rearrange("p (i d) -> p i d", i=16)
                vWv = vW.rearrange("p (i d) -> p i d", i=16)
                gatev = gate.rearrange("p (i k) -> p i k", i=16)
                MEMSET("pool", vSv[:, :, 64:65], 1.0, [RvS])
                MEMSET("pool", vWv[:, :, 64:65], 1.0, [RvW])
                CP("pool", Rt[:, 64:97], cb["c_rtail"][:, :], [Rcon], [RRt])
                MEMSET("pool", qT[64:128, :], 0.0, [RqT])
                MEMSET("pool", kTs[64:128, :], 0.0, [RkTs])
                DMA("pool", kTs[64:96, :], dr["c_E"][0:32, :], [], [RkTs])
                RqU = [Res("qU%d" % i_, after=[RqT]) for i_ in range(NT)]
                MEMSET("pool", kTw[64:128, :], 0.0, [RkTw])
                MEMSET("pool", kcT[64:128, :], 0.0, [RkcT])
                rope_b = []
                for k in range(2):
                    a1, r1 = alloc("cosb%d" % k, o, F32, 512)
                    a2, r2 = alloc("sinb%d" % k, o + 2048, F32, 512)
                    o += 4096
                    rope_b.append((a1, r1, a2, r2))
                A2, RA2 = alloc("A2", o, BF16, T)
                o += 4096
                hs, Rhs = alloc("hs", o, BF16, 256)
                o += 512
                sets = []
                for k in range(2):
                    d_ = {}
                    for nm, dt_, n_ in (("sq", BF16, 512), ("lnr", F32, 512), ("qn", F32, 512), ("qnb", BF16, 512),
                                        ("t1", F32, 512)):
                        a_, r_ = alloc("%s%d" % (nm, k), o, dt_, n_)
                        o += n_ * (4 if dt_ == F32 else 2)
                        d_[nm] = (a_, r_)
                    sets.append(d_)
                setrr = [0]

                def norm_rope(pb, Rpb, gain_ap, cosap, sinap, Rrope, outs, Rout, n):
                    st_ = sets[setrr[0] % 2]
                    setrr[0] += 1
                    sq, Rsq = st_["sq"]
                    lnr, Rlnr = st_["lnr"]
                    qn, Rqn = st_["qn"]
                    qnb, Rqnb = st_["qnb"]
                    t1, Rt1 = st_["t1"]
                    ACT(sq[:, 0:n], pb[:, 0:n], AF.Square, [Rpb], [Rsq])
                    p2, Rp2 = bank()
                    MM(p2[:, 0:n], cb["c_bd"][:], sq[:, 0:n], True, True, [Rsq, Rcon], [Rp2])
                    ACT(lnr[:, 0:n], p2[:, 0:n], AF.Ln, [Rp2], [Rlnr], scale=1.0 / 64, bias=EPS)
                    ACT(lnr[:, 0:n], lnr[:, 0:n], AF.Exp, [Rlnr], [Rlnr], scale=-0.5)
                    STT(qnb[:, 0:n], pb[:, 0:n], gain_ap, lnr[:, 0:n], ALU.mult, ALU.mult, [Rpb, Rcon, Rlnr], [Rqnb])
                    p3, Rp3 = bank()
                    MM(p3[:, 0:n], cb["c_prot"][:], qnb[:, 0:n], True, True, [Rqnb, Rcon], [Rp3])
                    TT("pool", t1[:, 0:n], qnb[:, 0:n], cosap, ALU.mult, [Rqnb] + Rrope, [Rt1])
                    TT("dve", qn[:, 0:n], p3[:, 0:n], sinap, ALU.mult, [Rp3] + Rrope, [Rqn])
                    for (oap, psl) in outs:
                        if len(oap.shape) == 3:
                            TT("dve", oap, qn[psl, 0:n].rearrange("p (a t) -> p a t", a=oap.shape[1]),
                               t1[psl, 0:n].rearrange("p (a t) -> p a t", a=oap.shape[1]), ALU.add, [Rqn, Rt1], [Rout])
                        else:
                            TT("dve", oap, qn[psl, 0:n], t1[psl, 0:n], ALU.add, [Rqn, Rt1], [Rout])

                wA, nsubA = wslot()
                wAv = wA[:, :].rearrange("p (kc n) -> p kc n", kc=8)
                RwA = [nsubA() for _ in range(5)]
                DMA("pool", wAv[:, :, 0:256], wcols(2048 + 256 * g, 256), [], [RwA[0]])
                for dup in range(2):
                    DMA("pool", wAv[:, :, 256 + 64 * dup:320 + 64 * dup], wcols(2816 + 64 * g, 64), [], [RwA[1 + dup]])
                    DMA("pool", wAv[:, :, 384 + 64 * dup:448 + 64 * dup], wcols(3072 + 64 * g, 64), [], [RwA[3 + dup]])
                wB, nsubB = wslot()
                wBv = wB[:, :].rearrange("p (kc n) -> p kc n", kc=8)
                RwB = [nsubB() for _ in range(7)]
                DMA("pool", wBv[:, :, 0:64], wcols(2944 + 64 * g, 64), [], [RwB[0]])
                DMA("pool", wBv[:, :, 64:128], wcols(3200 + 64 * g, 64), [], [RwB[1]])
                DMA("pool", wBv[:, :, 128:140], wcols(3328 + 12 * g, 12), [], [RwB[2]])
                for dup in range(2):
                    DMA("pool", wBv[:, :, 192 + 64 * dup:256 + 64 * dup], wcols(2560 + 64 * g, 64), [], [RwB[3 + dup]])
                    DMA("pool", wBv[:, :, 320 + 64 * dup:384 + 64 * dup], wcols(2688 + 64 * g, 64), [], [RwB[5 + dup]])

                for b in range(NB):
                    cosb, Rcosb, sinb, Rsinb = rope_b[b % 2]
                    DMA("sp", cosb, dr["c_cos"][:, bsl(b)], [], [Rcosb])
                    DMA("sp", sinb, dr["c_sin"][:, bsl(b)], [], [Rsinb])
                    lo, hi, al = slice(0, 64), slice(64, 128), slice(0, 128)
                    for (c0, gain_ap, dst, Rdst) in (
                            (0, pm["p_qg"][:, l:l + 1], [(qTv[0:64, 4 * b:4 * b + 4, 0, :], lo), (qTv[0:64, 4 * b:4 * b + 4, 1, :], hi)], RqT),
                            (128, pm["p_qg"][:, l:l + 1], [(qTv[0:64, 4 * b:4 * b + 4, 2, :], lo), (qTv[0:64, 4 * b:4 * b + 4, 3, :], hi)], RqT),
                            (256, pm["p_kg"][:, l, 1:2], [(kTs[0:64, bsl(b)], lo)], RkTs),
                            (384, pm["p_kg"][:, l, 2:3], [(kTw[0:64, bsl(b)], lo)], RkTw)):
                        pb, Rpb = bank()
                        for kc in range(8):
                            MM(pb[:, :], wAv[:, kc, c0:c0 + 128], xn[:, kc, bsl(b)], kc == 0, kc == 7,
                               RwA + [Rxn[kc][b]], [Rpb])
                        norm_rope(pb, Rpb, gain_ap, cosb, sinb, [Rcosb, Rsinb], dst, Rdst, 512)
                if NSA_STOP <= 1:
                    continue
                pgate, Rpgate = None, None
                for i0 in range(0, NT, 3):
                    nt_ = min(3, NT - i0)
                    pb, Rpb = bank()
                    for ti in range(nt_):
                        i = i0 + ti
                        for kc in range(8):
                            MM(pb[:, 140 * ti:140 * ti + 140], xn[:, kc, tsl(i)], wBv[:, kc, 0:140], kc == 0, kc == 7,
                               RwB[0:3] + [Rxn[kc][i // 4]], [Rpb], skip=True)
                    pbv = pb[:, 0:140 * nt_].rearrange("p (i d) -> p i d", i=nt_)
                    CP("act", vSv[:, i0:i0 + nt_, 0:64], pbv[:, :, 0:64], [Rpb], [RvS])
                    CP("dve", vWv[:, i0:i0 + nt_, 0:64], pbv[:, :, 64:128], [Rpb], [RvW])
                    ACT(gatev[:, i0:i0 + nt_, :], pbv[:, :, 128:140], AF.Exp, [Rpb], [Rgate], scale=-1.0)
                TS("dve", gate, gate, 1.0, None, ALU.add, None, [Rgate], [Rgate])
                RECIP(gate, gate, [Rgate], [Rgate])
                if NSA_STOP <= 2:
                    continue
                for kv in range(2):
                    c0 = 192 if kv == 0 else 320
                    for b in range(NB):
                        pb, Rpb = bank()
                        for kc in range(8):
                            MM(pb[:, :], wBv[:, kc, c0:c0 + 128], xn[:, kc, bsl(b)], kc == 0, kc == 7,
                               RwB[3:7] + [Rxn[kc][b]], [Rpb])
                        CP("act", A2[0:64, bsl(b)], pb[0:64, :], [Rpb], [RA2])
                        if b == 0:
                            CP("dve", A2[64:128, 0:511], pb[64:128, 1:512], [Rpb], [RA2])
                        else:
                            CP("dve", A2[64:128, 512 * b - 1:512 * b + 511], pb[64:128, :], [Rpb], [RA2])
                    w1 = dr["cmp_w1_k" if kv == 0 else "cmp_w1_v"][l]
                    w2 = dr["cmp_w2_k" if kv == 0 else "cmp_w2_v"][l]
                    wt, nsub = wslot()
                    Rw = nsub()
                    wtv = wt[:, :].rearrange("p (c n) -> p c n", c=16)
                    DMA("pool", wtv, w1.rearrange("(c p) n -> p c n", p=128), [], [Rw])
                    w2t, Rw2 = alloc("w2t", o, BF16, 256)
                    w2v = w2t.rearrange("p (jc n) -> p jc n", jc=2)
                    for dup in range(2):
                        DMA("pool", w2v[:, :, 64 * dup:64 * dup + 64], w2.rearrange("(jc p) n -> p jc n", p=128), [], [Rw2])
                    hsv = hs.rearrange("p (jc n) -> p jc n", jc=2)
                    sgc, Rsgc = alloc("sgc", o + 512, F32, 128)
                    for jc in range(2):
                        pb, Rpb = bank()
                        for c in range(16):
                            MM(pb[:, 0:127], wtv[:, c, 128 * jc:128 * jc + 128], A2[:, 2 * c:2 * c + 16 * 126 + 1:16],
                               c == 0, c == 15, [Rw, RA2], [Rpb])
                        ACT(sgc[:, 0:127], pb[:, 0:127], AF.Exp, [Rpb, Rcst], [Rsgc], scale=-1.0, bias=ncstT[:, l, kv, jc:jc + 1])
                        TS("dve", sgc[:, 0:127], sgc[:, 0:127], 1.0, None, ALU.add, None, [Rsgc], [Rsgc])
                        RECIP(sgc[:, 0:127], sgc[:, 0:127], [Rsgc], [Rsgc])
                        STT(hsv[:, jc, 0:127], pb[:, 0:127], cstT[:, l, kv, jc:jc + 1], sgc[:, 0:127], ALU.add, ALU.mult,
                            [Rpb, Rcst, Rsgc], [Rhs])
                    if kv == 0:
                        pb, Rpb = bank()
                        for jc in range(2):
                            MM(pb[:, 0:127], w2v[:, jc, :], hsv[:, jc, 0:127], jc == 0, jc == 1, [Rw2, Rhs], [Rpb])
                        norm_rope(pb, Rpb, pm["p_kg"][:, l, 0:1], cb["c_cosC"][:], cb["c_sinC"][:], [Rcon],
                                  [(kcT[0:64, 0:127], slice(0, 64))], RkcT, 127)
                    else:
                        pb, Rpb = bank()
                        for jc in range(2):
                            MM(pb[0:127, 0:64], hsv[:, jc, 0:127], w2v[:, jc, 0:64], jc == 0, jc == 1, [Rw2, Rhs], [Rpb])
                        CP("act", Rt[0:127, 0:64], pb[0:127, 0:64], [Rpb], [RRt])
                if dbg and g == 0:
                    dump("nsa_qT", qT, [RqT])
                    dump("nsa_kTs", kTs, [RkTs])
                    dump("nsa_kcT", kcT, [RkcT])
                    dump("nsa_Rt", Rt, [RRt])
                    dump("nsa_gate", gate, [Rgate])
                    dump("nsa_vS", vS, [RvS])

                if NSA_STOP <= 3:
                    continue
                o = TMP0 + 30 * 1024
                eTs = []
                for k in range(3):
                    a_, r_ = alloc("eT%d" % k, o, BF16, 512)
                    o += 1024
                    eTs.append((a_, r_))
                err = [0]
                n2 = []
                for k in range(2):
                    d_ = {}
                    for nm, dt_, n_ in (("oacc", F32, 256), ("obf", BF16, 256), ("negT", BF16, 128), ("rs", F32, 4),
                                        ("coef", F32, 4), ("imp", F32, 32), ("sc", F32, 32), ("sc2", F32, 32),
                                        ("m8", F32, 16), ("negm", BF16, 128)):
                        a_, r_ = alloc("%s%d" % (nm, k), o, dt_, n_)
                        o += n_ * (4 if dt_ == F32 else 2)
                        d_[nm] = (a_, r_)
                    n2.append(d_)
                    MEMSET("pool", d_["negm"][0], 0.0, [d_["negm"][1]])
                hg_of_hb = [0, 1, 2, 3]

                def qk(pb, Rpb, kT, RkT, ksl, nk, i):
                    MM(pb[0:nk, :], kT[:, ksl], qTv[:, i, :, :], True, False, [RkT, RqT, RqU[i]], [Rpb], skip=True)

                def bc4(ap):
                    return ap.unsqueeze(1).to_broadcast([ap.shape[0], 4, ap.shape[1]])

                for i in range(NT):
                    nb_ = n2[i % 2]
                    oacc, Roacc = nb_["oacc"]
                    obf, Robf = nb_["obf"]
                    negT, RnegT = nb_["negT"]
                    rs, Rrs = nb_["rs"]
                    coef, Rcoef = nb_["coef"]
                    imp, Rimp = nb_["imp"]
                    sc, Rsc = nb_["sc"]
                    sc2, Rsc2 = nb_["sc2"]
                    m8, Rm8 = nb_["m8"]
                    negm, Rnegm = nb_["negm"]
                    oaccv = oacc.rearrange("p (h d) -> p h d", h=4)
                    P.tag = "nsa_cmp"
                    nv = min(127, 8 * i + 7)
                    pb, Rpb = bank()
                    qk(pb, Rpb, kcT, RkcT, slice(0, nv), nv, i)
                    MM(pb[0:nv, :], ident_b[0:nv, 0:nv], bc4(cb["c_cmpneg"][0:nv, tsl(i)]), False, True, [Rcon], [Rpb], skip=True)
                    eT, ReT = eTs[err[0] % 3]
                    err[0] += 1
                    ACT(eT[0:nv, :], pb[0:nv, :], AF.Exp, [Rpb], [ReT], scale=0.125)
                    poc, Rpoc = bank()
                    for hb in range(4):
                        MM(poc[:, 128 * hb:128 * hb + 97], eT[0:nv, 128 * hb:128 * hb + 128], Rt[0:nv, 0:97], hb == 0, hb == 3,
                           [ReT, RRt], [Rpoc], skip=True)
                    pocv = poc[:, :].rearrange("p (h c) -> p h c", h=4)
                    ACT(rs, pocv[:, :, 64], AF.Ln, [Rpoc], [Rrs], bias=1e-18)
                    ACT(rs, rs, AF.Exp, [Rrs], [Rrs], scale=-1.0)
                    TT("dve", coef, rs, gatev[:, i, 0:10:3], ALU.mult, [Rrs, Rgate], [Rcoef])
                    for hb in range(4):
                        hg_ = hg_of_hb[hb]
                        TS("dve", oaccv[:, hg_, :], pocv[:, hb, 0:64], coef[:, hb:hb + 1], None, ALU.mult, None,
                           [Rpoc, Rcoef], [Roacc])
                    if i >= 8 and NSA_STOP > 4:
                        TS("dve", imp, pocv[:, 0, 65:97], rs[:, 0:1], None, ALU.mult, None, [Rpoc, Rrs], [Rimp])
                        for hb in range(1, 4):
                            STT(imp, pocv[:, hb, 65:97], rs[:, hb:hb + 1], imp, ALU.mult, ALU.add, [Rpoc, Rrs, Rimp], [Rimp])
                        TT("dve", sc, imp, cb["c_C"][:, i - 8, :], ALU.add, [Rimp, Rcon], [Rsc])
                        P.op("dve", lambda e, m8=m8, sc=sc: e.max(out=m8[:, 0:8], in_=sc), [Rsc], [Rm8])
                        P.op("dve", lambda e, m8=m8, sc=sc, sc2=sc2: e.match_replace(out=sc2, in_to_replace=m8[:, 0:8],
                                                                                     in_values=sc, imm_value=-3e38),
                             [Rsc, Rm8], [Rsc2])
                        P.op("dve", lambda e, m8=m8, sc2=sc2: e.max(out=m8[:, 8:16], in_=sc2), [Rsc2], [Rm8])
                        TS("dve", sc2, sc, m8[:, 15:16], None, ALU.is_ge, None, [Rsc, Rm8], [Rsc2])
                        TS("dve", negm[:, 64:96], sc2, -1.0, -NEGM, ALU.add, ALU.mult, [Rsc2], [Rnegm])
                        ptb, Rptb = bank()
                        ptv = ptb[:, :].bitcast(BF16)
                        TR(ptv[:, 0:128], negm, ident_b[:], [Rnegm, Rcon], [Rptb])
                        CP("act", qTv[64:96, i, :, :], ptv[64:96, 0:128].unsqueeze(1).to_broadcast([32, 4, 128]), [Rptb],
                           [RqU[i]])
                    for br in ((1, 2) if NSA_STOP > 6 else ((1,) if NSA_STOP > 5 else ())):
                        P.tag = "nsa_slc" if br == 1 else "nsa_swa"
                        kT, RkT = (kTs, RkTs) if br == 1 else (kTw, RkTw)
                        vv, Rvv = (vSv, RvS) if br == 1 else (vWv, RvW)
                        kts = list(range(0, i + 1)) if br == 1 else list(range(max(0, i - 4), i + 1))
                        pov, Rpov = rbank()
                        for n_, kt_ in enumerate(kts):
                            pb, Rpb = bank()
                            qk(pb, Rpb, kT, RkT, tsl(kt_), 128, i)
                            if kt_ == i:
                                MM(pb[:, :], ident_b[:], bc4(cb["c_triA"][:]), False, False, [Rcon], [Rpb], skip=True)
                            if br == 2 and kt_ == i - 4:
                                MM(pb[:, :], ident_b[:], bc4(cb["c_triB"][:]), False, False, [Rcon], [Rpb], skip=True)
                            eT, ReT = eTs[err[0] % 3]
                            err[0] += 1
                            ACT(eT, pb[:, :], AF.Exp, [Rpb], [ReT], scale=0.125)
                            for hb in range(4):
                                MM(pov[:, 128 * hb:128 * hb + 65], eT[:, 128 * hb:128 * hb + 128], vv[:, kt_, :],
                                   n_ == 0 and hb == 0, n_ == len(kts) - 1 and hb == 3, [ReT, Rvv], [Rpov], skip=True)
                        povv = pov[:, :].rearrange("p (h c) -> p h c", h=4)
                        ACT(rs, povv[:, :, 64], AF.Ln, [Rpov], [Rrs])
                        ACT(rs, rs, AF.Exp, [Rrs], [Rrs], scale=-1.0)
                        TT("dve", coef, rs, gatev[:, i, br:br + 10:3], ALU.mult, [Rrs, Rgate], [Rcoef])
                        for hb in range(4):
                            hg_ = hg_of_hb[hb]
                            if br == 1:
                                STT(oaccv[:, hg_, :], povv[:, hb, 0:64], coef[:, hb:hb + 1], oaccv[:, hg_, :], ALU.mult, ALU.add,
                                    [Rpov, Rcoef, Roacc], [Roacc])
                            else:
                                STT(obf[:, 64 * hg_:64 * hg_ + 64], povv[:, hb, 0:64], coef[:, hb:hb + 1], oaccv[:, hg_, :],
                                    ALU.mult, ALU.add, [Rpov, Rcoef, Roacc], [Robf])
                    if NSA_STOP <= 6:
                        continue
                    P.tag = "nsa_fin"
                    ptb, Rptb = bank()
                    ptv = ptb[:, :].bitcast(BF16)
                    for cc in range(2):
                        TR(ptv[:, 128 * cc:128 * cc + 128], obf[:, 128 * cc:128 * cc + 128], ident_b[:], [Robf, Rcon], [Rptb])
                    b = i // 4
                    for cc in range(2):
                        kc = 2 * g + cc
                        CP("act", AR[:, kc * T + 128 * i:kc * T + 128 * i + 128], ptv[:, 128 * cc:128 * cc + 128], [Rptb],
                           [Rmix[kc][b]])

        def ffn(l):
            P.tag = "ffn"
            P.act_barrier()
            wi = dr["w_ffn_in"][l]
            wo = dr["w_ffn_out"][l]
            wiv = wi.rearrange("(kc p) (two n) -> p kc two n", p=128, two=2)
            for hf in range(2):
                actT, Rab = alloc("actT", 0, BF16, 22 * 1024)
                Ract = [[Res("act%d_%d" % (j, bb), after=[Rab]) for bb in range(2)] for j in range(22)]
                actv = actT.rearrange("p (j t) -> p j t", j=22)
                sgs = []
                for k in range(3):
                    a_, r_ = alloc("fsg%d" % k, 52 * 1024 + 1024 * k, BF16, 512)
                    sgs.append((a_, r_))
                srr = 0
                for j in range(22):
                    wt, nsub = whalf()
                    wtv = wt.rearrange("p (kc two n) -> p kc two n", kc=8, two=2)
                    Rw2_ = [nsub(), nsub()]
                    for two_ in range(2):
                        DMA("pool", wtv[:, :, two_, :], wiv[:, :, two_, 128 * j:128 * j + 128], [], [Rw2_[two_]])
                    for bb in range(2):
                        b = 2 * hf + bb
                        pg, Rpg = bank()
                        for kc in range(8):
                            MM(pg[:, :], wtv[:, kc, 0, :], xn[:, kc, bsl(b)], kc == 0, kc == 7, [Rw2_[0], Rxn[kc][b]], [Rpg])
                        pu, Rpu = bank()
                        for kc in range(8):
                            MM(pu[:, :], wtv[:, kc, 1, :], xn[:, kc, bsl(b)], kc == 0, kc == 7, [Rw2_[1], Rxn[kc][b]], [Rpu])
                        sg, Rsg = sgs[srr % 3]
                        srr += 1
                        ACT(sg, pg[:, :], AF.Silu, [Rpg], [Rsg])
                        TT("dve", actv[:, j, 512 * bb:512 * bb + 512], pu[:, :], sg, ALU.mult, [Rpu, Rsg], [Ract[j][bb]])
                for co in range(8):
                    wt, nsub = wslot()
                    Rw = nsub()
                    wtv = wt[:, 0:22 * 128].rearrange("p (kc n) -> p kc n", kc=22)
                    DMA("pool", wtv, wo[:, 128 * co:128 * co + 128].rearrange("(kc p) n -> p kc n", p=128), [], [Rw])
                    for bb in range(2):
                        b = 2 * hf + bb
                        pb, Rpb = bank()
                        for kc in range(22):
                            MM(pb[:, :], wtv[:, kc, :], actv[:, kc, 512 * bb:512 * bb + 512], kc == 0, kc == 21,
                               [Rw, Ract[kc][bb]], [Rpb])
                        TT("dve", hT[:, co, bsl(b)], pb[:, :], hT[:, co, bsl(b)], ALU.add, [Rpb, RhT[co][b]], [RhT[co][b]])
                allact = Res("actall", after=[Ract[j][bb] for j in range(22) for bb in range(2)])
                rereg(0, 45056, allact)
            P.act_barrier()

        def compute_cst():
            for l in range(nlayers):
                for kv in range(2):
                    w1 = dr["cmp_w1_k" if kv == 0 else "cmp_w1_v"][l]
                    wt, nsub = wslot()
                    Rw = nsub()
                    wtv = wt[:, :].rearrange("p (c n) -> p c n", c=16)
                    DMA("pool", wtv, w1.rearrange("(c p) n -> p c n", p=128), [], [Rw])
                    pb, Rpb = bank()
                    for jc in range(2):
                        for c in range(16):
                            MM(pb[:, jc:jc + 1], wtv[:, c, 128 * jc:128 * jc + 128], peb[:, l, kv, c:c + 1], c == 0, c == 15,
                               [Rw, Rcon], [Rpb], skip=True)
                    CP("dve", cstT[:, l, kv, :], pb[:, 0:2], [Rpb], [Rcst])
                    TS("dve", ncstT[:, l, kv, :], pb[:, 0:2], -1.0, None, ALU.mult, None, [Rpb], [Rcst])

        compute_cst()
        for s in range(nseq):
            load_x(s)
            for l in range(nlayers):
                rmsnorm(l, 0, TMP0 + 40 * 1024)
                if dbg and s == 0 and l == 0:
                    dump("xn", xn[:, :, :], [Rxn[c][b] for c in range(8) for b in range(NB)])
                if "hg" in phases:
                    base = ares("mixbase", 0, 16384)
                    Rmix = [[Res("mix%d_%d" % (k, b), after=[base]) for b in range(NB)] for k in range(4)]
                    hgrn2(l, Rmix)
                    if dbg and s == 0 and l == 0:
                        dump("mix_hg", AR[:, 0:4 * T], [Rmix[k][b] for k in range(4) for b in range(NB)])
                    outproj(l, 0, Rmix)
                    allm = Res("mixall", after=[Rmix[k][b] for k in range(4) for b in range(NB)])
                    rereg(0, 16384, allm)
                if "nsa" in phases:
                    base = ares("mixbase", 0, 16384)
                    Rmix = [[Res("mix%d_%d" % (k, b), after=[base]) for b in range(NB)] for k in range(4)]
                    nsa(l, Rmix)
                    if dbg and s == 0 and l == 0:
                        dump("mix_nsa", AR[:, 0:4 * T], [Rmix[k][b] for k in range(4) for b in range(NB)])
                    if NSA_STOP > 6:
                        outproj(l, 1, Rmix)
                    allm = Res("mixall", after=[Rmix[k][b] for k in range(4) for b in range(NB)])
                    rereg(0, 16384, allm)
                if dbg and s == 0 and l == 0:
                    dump("h_mid", hT[:, :, :], [RhT[c][b] for c in range(8) for b in range(NB)])
                if "ffn" in phases:
                    rmsnorm(l, 1, TMP0 + 40 * 1024)
                    ffn(l)
            store_out(s)
        P.final_waits = out_dmas
        P.emit(sems)
        build_program.stats = {s: len(P.streams[s]) for s in Prog.STREAMS}
        build_program.stats["waits"] = P.nwaits
        build_program.stats["makespan_us"] = P.makespan / 1e3
        build_program.stats["busy_us"] = getattr(P, "busy", None)
        build_program.tagcost = P.tagcost
        build_program.critsum = getattr(P, "critsum", {})
    return nc


_CACHE = {}


def kernel(**inputs):
    x = np.ascontiguousarray(np.asarray(inputs["x"], dtype=np.float32))
    B = x.shape[0]
    per = B // NCORES
    if "nc" not in _CACHE:
        _CACHE["nc"] = build_program(nseq=per)
        _CACHE["consts"] = host_consts()
    nc = _CACHE["nc"]
    common = {}
    for k in WEIGHT_SHAPES:
        common[k] = np.ascontiguousarray(np.asarray(inputs[k], dtype=np.float32))
    common.update(host_params(inputs))
    common.update(_CACHE["consts"])
    in_maps = []
    for c in range(NCORES):
        m = dict(common)
        m["x"] = x[c * per:(c + 1) * per]
        in_maps.append(m)
    res = run_bass_kernel_spmd(nc, in_maps, core_ids=list(range(NCORES)))
    outs = [np.asarray(r["out"], dtype=np.float32) for r in res.results]
    return np.concatenate(outs, axis=0)
```
